# Optimizing a Trainium2 kernel written in Bass

```python
import jax
import jax.numpy as jnp
from jax import lax
import numpy as np

D_MODEL = 2048
BATCH = 2
SEQ = 4096
DEPTH = 4

N_A_LAYERS = DEPTH // 2
RET_HEADS = 8
RET_DK = D_MODEL // RET_HEADS
RET_DV = 2 * D_MODEL // RET_HEADS
RET_CHUNK = 128
RET_THETA_BASE = 10000.0
NSA_HEADS = 16
NSA_GROUPS = 4
NSA_HPG = NSA_HEADS // NSA_GROUPS
NSA_DV = D_MODEL // NSA_HEADS
NSA_DK = 3 * NSA_DV // 2
CMP_LEN = 32
CMP_STRIDE = 16
CMP_HID = 2 * NSA_DV
SEL_LEN = 64
SEL_TOPK = 16
WINDOW = 512
SEL_QBLOCK = 32
WIN_QBLOCK = 128
N_BRANCH = 3
ALPHA = (2.0 * DEPTH) ** 0.25
BETA = (8.0 * DEPTH) ** -0.25
NEG_INF = -1e30
FORCE_SCORE = 1e9
LN_EPS = 1e-5

kernel_name = 'hybrid_retnet_nsa_yoco_deepnorm'


def layer_norm(x, g, b):
    xf = x.astype(jnp.float32)
    mu = xf.mean(-1, keepdims=True)
    var = jnp.mean(jnp.square(xf - mu), -1, keepdims=True)
    return ((xf - mu) * lax.rsqrt(var + LN_EPS) * g + b).astype(x.dtype)


def rotate_pairs(t, positions):
    d = t.shape[-1]
    inv_freq = 1.0 / (RET_THETA_BASE ** jnp.linspace(0.0, 1.0, d // 2, dtype=jnp.float32))
    ang = positions.astype(jnp.float32)[:, :, None, None] * inv_freq
    cos, sin = jnp.cos(ang), jnp.sin(ang)
    pair = t.reshape(*t.shape[:-1], d // 2, 2)
    even, odd = pair[..., 0], pair[..., 1]
    return jnp.stack([even * cos - odd * sin, even * sin + odd * cos], axis=-1).reshape(t.shape)


def retention_mixer(x, positions, w_in, w_out):
    bsz, seq, _ = x.shape
    h, dk, dv, c = RET_HEADS, RET_DK, RET_DV, RET_CHUNK
    n_chunk = seq // c
    f32 = jnp.float32
    proj = x @ w_in
    q, k, v, z = jnp.split(proj, [h * dk, 2 * h * dk, 2 * h * dk + h * dv], axis=-1)
    q = rotate_pairs(q.reshape(bsz, seq, h, dk).astype(f32), positions)
    k = rotate_pairs(k.reshape(bsz, seq, h, dk).astype(f32), positions) * (dk ** -0.5)
    v = v.reshape(bsz, seq, h, dv).astype(f32)

    def to_chunks(t):
        return t.reshape(bsz, n_chunk, c, h, -1).transpose(1, 0, 3, 2, 4)

    log_gamma = jnp.log1p(-jnp.exp2(-5.0 - jnp.arange(h, dtype=f32)))
    idx = jnp.arange(c, dtype=f32)
    rel = idx[:, None] - idx[None, :]
    decay_intra = jnp.where(rel >= 0, jnp.exp(log_gamma[:, None, None] * jnp.maximum(rel, 0.0)), 0.0)
    decay_q = jnp.exp(log_gamma[:, None] * (idx + 1.0))[None, :, :, None]
    decay_k = jnp.exp(log_gamma[:, None] * (c - 1.0 - idx))[None, :, :, None]
    decay_chunk = jnp.exp(log_gamma * c)[None, :, None, None]

    def step(state, qkv):
        qc, kc, vc = qkv
        scores = jnp.einsum('bhcd,bhmd->bhcm', qc, kc) * decay_intra
        out = (jnp.einsum('bhcm,bhme->bhce', scores, vc)
               + jnp.einsum('bhcd,bhde->bhce', qc, state) * decay_q)
        state = state * decay_chunk + jnp.einsum('bhcd,bhce->bhde', kc * decay_k, vc)
        return state, out

    state0 = jnp.zeros((bsz, h, dk, dv), f32)
    _, o = lax.scan(step, state0, (to_chunks(q), to_chunks(k), to_chunks(v)))
    o = o.transpose(1, 0, 3, 2, 4).reshape(bsz, seq, h, dv)
    mu = o.mean(-1, keepdims=True)
    var = jnp.mean(jnp.square(o - mu), -1, keepdims=True)
    o = ((o - mu) * lax.rsqrt(var + LN_EPS)).reshape(bsz, seq, h * dv).astype(x.dtype)
    return (o * jax.nn.silu(z)) @ w_out


def nsa_shared_kv(hs, w_kv, pe_k, pe_v, w_ck1, w_ck2, w_cv1, w_cv2):
    bsz, seq, _ = hs.shape
    g, dk, dv = NSA_GROUPS, NSA_DK, NSA_DV
    kv = hs @ w_kv
    sizes = [g * dk, g * dv] * N_BRANCH
    parts = jnp.split(kv, np.cumsum(sizes)[:-1].tolist(), axis=-1)
    k_cmp, v_cmp, k_sel, v_sel, k_win, v_win = [p.reshape(bsz, seq, g, -1).transpose(0, 2, 1, 3) for p in parts]
    n_cmp = (seq - CMP_LEN) // CMP_STRIDE + 1
    tok = jnp.arange(n_cmp)[:, None] * CMP_STRIDE + jnp.arange(CMP_LEN)[None, :]

    def compress(t, pe, w1, w2):
        blocks = t[:, :, tok] + pe
        flat = blocks.reshape(bsz, g, n_cmp, -1)
        return jax.nn.silu(flat @ w1) @ w2

    kc = compress(k_cmp, pe_k, w_ck1, w_ck2)
    vc = compress(v_cmp, pe_v, w_cv1, w_cv2)
    return (kc, vc, k_sel, v_sel, k_win, v_win)


def nsa_mixer(x, shared, w_in, w_out):
    kc, vc, ks, vs, kw, vw = shared
    bsz, seq, _ = x.shape
    g, r, dk, dv, h = NSA_GROUPS, NSA_HPG, NSA_DK, NSA_DV, NSA_HEADS
    f32 = jnp.float32
    scale = dk ** -0.5
    proj = x @ w_in
    q, z_cmp, z_sel, z_win, gate = jnp.split(
        proj, [h * dk, h * dk + h * dv, h * dk + 2 * h * dv, h * dk + 3 * h * dv], axis=-1)
    q = q.reshape(bsz, seq, g, r, dk).transpose(0, 2, 3, 1, 4)
    t = jnp.arange(seq)

    n_cmp = kc.shape[2]
    blk_end = jnp.arange(n_cmp) * CMP_STRIDE + CMP_LEN - 1
    cmask = blk_end[None, :] <= t[:, None]
    s_cmp = jnp.einsum('bgrsd,bgnd->bgrsn', q, kc).astype(f32) * scale
    p_cmp = jax.nn.softmax(jnp.where(cmask, s_cmp, NEG_INF), axis=-1) * cmask.any(-1, keepdims=True).astype(f32)
    o_cmp = jnp.einsum('bgrsn,bgne->bgrse', p_cmp.astype(vc.dtype), vc)

    n_sel = seq // SEL_LEN
    a, b = SEL_LEN // CMP_STRIDE, CMP_LEN // CMP_STRIDE
    span = a + b - 2
    diff = jnp.arange(n_cmp)[:, None] - a * jnp.arange(n_sel)[None, :]
    overlap = jnp.where((diff >= 0) & (diff <= span),
                        jnp.minimum(jnp.minimum(diff, span - diff), min(a, b) - 1) + 1, 0).astype(f32)
    p_sel = jnp.einsum('bgrsn,nj->bgsj', p_cmp, overlap)
    blk = jnp.arange(n_sel)[None, :]
    cur = (t // SEL_LEN)[:, None]
    forced = (blk == 0) | (blk == cur) | (blk == cur - 1)
    score = jnp.where(forced, FORCE_SCORE, jnp.where(blk <= cur, p_sel, -1.0))
    n_top = min(SEL_TOPK, n_sel)
    _, sel_idx = lax.top_k(score, n_top)

    ks_blk = ks.reshape(bsz, g, n_sel, SEL_LEN, dk)
    vs_blk = vs.reshape(bsz, g, n_sel, SEL_LEN, dv)
    n_qb = seq // SEL_QBLOCK
    q_blocks = q.reshape(bsz, g, r, n_qb, SEL_QBLOCK, dk).transpose(3, 0, 1, 2, 4, 5)
    idx_blocks = sel_idx.reshape(bsz, g, n_qb, SEL_QBLOCK, n_top).transpose(2, 0, 1, 3, 4)
    q_starts = jnp.arange(n_qb) * SEL_QBLOCK
    bi = jnp.arange(bsz)[:, None, None, None]
    gi = jnp.arange(g)[None, :, None, None]
    tok_in_blk = jnp.arange(SEL_LEN)

    def sel_block(args):
        qb, ib, q0 = args
        kg = ks_blk[bi, gi, ib].reshape(bsz, g, SEL_QBLOCK, n_top * SEL_LEN, dk)
        vg = vs_blk[bi, gi, ib].reshape(bsz, g, SEL_QBLOCK, n_top * SEL_LEN, dv)
        kpos = (ib[..., None] * SEL_LEN + tok_in_blk).reshape(bsz, g, SEL_QBLOCK, n_top * SEL_LEN)
        qpos = q0 + jnp.arange(SEL_QBLOCK)
        mask = (kpos <= qpos[None, None, :, None])[:, :, None]
        s = jnp.einsum('bgrqd,bgqkd->bgrqk', qb, kg).astype(f32) * scale
        p = jax.nn.softmax(jnp.where(mask, s, NEG_INF), axis=-1)
        return jnp.einsum('bgrqk,bgqke->bgrqe', p.astype(vg.dtype), vg)

    o_sel = lax.map(sel_block, (q_blocks, idx_blocks, q_starts))
    o_sel = o_sel.transpose(1, 2, 3, 0, 4, 5).reshape(bsz, g, r, seq, dv)

    kw_pad = jnp.pad(kw, ((0, 0), (0, 0), (WINDOW, 0), (0, 0)))
    vw_pad = jnp.pad(vw, ((0, 0), (0, 0), (WINDOW, 0), (0, 0)))
    span_k = WIN_QBLOCK + WINDOW

    def win_block(q0):
        qb = lax.dynamic_slice_in_dim(q, q0, WIN_QBLOCK, axis=3)
        kb = lax.dynamic_slice_in_dim(kw_pad, q0, span_k, axis=2)
        vb = lax.dynamic_slice_in_dim(vw_pad, q0, span_k, axis=2)
        qpos = q0 + jnp.arange(WIN_QBLOCK)
        kpos = q0 - WINDOW + jnp.arange(span_k)
        dist = qpos[:, None] - kpos[None, :]
        mask = (dist >= 0) & (dist < WINDOW) & (kpos[None, :] >= 0)
        s = jnp.einsum('bgrqd,bgkd->bgrqk', qb, kb).astype(f32) * scale
        p = jax.nn.softmax(jnp.where(mask, s, NEG_INF), axis=-1)
        return jnp.einsum('bgrqk,bgke->bgrqe', p.astype(vb.dtype), vb)

    o_win = lax.map(win_block, jnp.arange(seq // WIN_QBLOCK) * WIN_QBLOCK)
    o_win = o_win.transpose(1, 2, 3, 0, 4, 5).reshape(bsz, g, r, seq, dv)

    def heads(o):
        return o.transpose(0, 3, 1, 2, 4).reshape(bsz, seq, h, dv).astype(x.dtype)

    def gate_path(zz):
        return jax.nn.silu(zz.reshape(bsz, seq, h, dv))

    gates = jax.nn.sigmoid(gate.reshape(bsz, seq, N_BRANCH, h))[..., None]
    mixed = (gates[:, :, 0] * heads(o_cmp) * gate_path(z_cmp)
             + gates[:, :, 1] * heads(o_sel) * gate_path(z_sel)
             + gates[:, :, 2] * heads(o_win) * gate_path(z_win))
    return mixed.reshape(bsz, seq, h * dv) @ w_out


def setup_inputs(seed: int = 0) -> dict:
    key = jax.random.key(seed)
    keys = iter(jax.random.split(key, 64))
    f32 = jnp.float32

    def dense(fan_in, fan_out, scale=1.0):
        return jax.random.normal(next(keys), (fan_in, fan_out), f32) * (scale * fan_in ** -0.5)

    def norm_pair():
        gain = 1.0 + 0.05 * jax.random.normal(next(keys), (D_MODEL,), f32)
        bias = 0.02 * jax.random.normal(next(keys), (D_MODEL,), f32)
        return gain, bias

    inputs = {}
    inputs['x'] = jax.random.normal(next(keys), (BATCH, SEQ, D_MODEL), f32)
    offset = jax.random.randint(next(keys), (BATCH, 1), 0, 1024, dtype=jnp.int32)
    inputs['positions'] = offset + jnp.arange(SEQ, dtype=jnp.int32)[None, :]
    hr = RET_HEADS
    for layer in range(N_A_LAYERS):
        inputs[f'ret_w_in_{layer}'] = jnp.concatenate([
            dense(D_MODEL, hr * RET_DK), dense(D_MODEL, hr * RET_DK),
            dense(D_MODEL, hr * RET_DV, BETA), dense(D_MODEL, hr * RET_DV)], axis=1)
        inputs[f'ret_w_out_{layer}'] = dense(hr * RET_DV, D_MODEL, BETA)
        gain, bias = norm_pair()
        inputs[f'ln_g_{layer}'] = gain
        inputs[f'ln_b_{layer}'] = bias
    cols = []
    for _ in range(N_BRANCH):
        cols += [dense(D_MODEL, NSA_GROUPS * NSA_DK), dense(D_MODEL, NSA_GROUPS * NSA_DV, BETA)]
    inputs['nsa_w_kv'] = jnp.concatenate(cols, axis=1)
    inputs['nsa_pe_k'] = 0.1 * jax.random.normal(next(keys), (CMP_LEN, NSA_DK), f32)
    inputs['nsa_pe_v'] = 0.1 * jax.random.normal(next(keys), (CMP_LEN, NSA_DV), f32)
    inputs['nsa_w_ck1'] = dense(CMP_LEN * NSA_DK, CMP_HID)
    inputs['nsa_w_ck2'] = dense(CMP_HID, NSA_DK)
    inputs['nsa_w_cv1'] = dense(CMP_LEN * NSA_DV, CMP_HID)
    inputs['nsa_w_cv2'] = dense(CMP_HID, NSA_DV)
    hn = NSA_HEADS
    for layer in range(N_A_LAYERS, DEPTH):
        inputs[f'nsa_w_in_{layer}'] = jnp.concatenate([
            dense(D_MODEL, hn * NSA_DK), dense(D_MODEL, hn * NSA_DV), dense(D_MODEL, hn * NSA_DV),
            dense(D_MODEL, hn * NSA_DV), dense(D_MODEL, N_BRANCH * hn)], axis=1)
        inputs[f'nsa_w_out_{layer}'] = dense(hn * NSA_DV, D_MODEL, BETA)
        gain, bias = norm_pair()
        inputs[f'ln_g_{layer}'] = gain
        inputs[f'ln_b_{layer}'] = bias
    return inputs


def reference(x, positions, ret_w_in_0, ret_w_out_0, ln_g_0, ln_b_0, ret_w_in_1, ret_w_out_1, ln_g_1, ln_b_1,
              nsa_w_kv, nsa_pe_k, nsa_pe_v, nsa_w_ck1, nsa_w_ck2, nsa_w_cv1, nsa_w_cv2,
              nsa_w_in_2, nsa_w_out_2, ln_g_2, ln_b_2, nsa_w_in_3, nsa_w_out_3, ln_g_3, ln_b_3):
    mixer_params = [(ret_w_in_0, ret_w_out_0), (ret_w_in_1, ret_w_out_1),
                    (nsa_w_in_2, nsa_w_out_2), (nsa_w_in_3, nsa_w_out_3)]
    norm_params = [(ln_g_0, ln_b_0), (ln_g_1, ln_b_1), (ln_g_2, ln_b_2), (ln_g_3, ln_b_3)]
    shared = None
    for layer in range(DEPTH):
        w_in, w_out = mixer_params[layer]
        if layer < N_A_LAYERS:
            y = retention_mixer(x, positions, w_in, w_out)
        else:
            if layer == N_A_LAYERS:
                shared = nsa_shared_kv(x, nsa_w_kv, nsa_pe_k, nsa_pe_v, nsa_w_ck1, nsa_w_ck2, nsa_w_cv1, nsa_w_cv2)
            y = nsa_mixer(x, shared, w_in, w_out)
        gain, bias = norm_params[layer]
        x = layer_norm(ALPHA * x + y, gain, bias)
    return x
```

```python
import numpy as np
import ml_dtypes
from contextlib import ExitStack
import concourse.bass as bass
import concourse.mybir as mybir
from concourse.bass_utils import run_bass_kernel_spmd

F32 = mybir.dt.float32
BF16 = mybir.dt.bfloat16
I32 = mybir.dt.int32
ALU = mybir.AluOpType
AF = mybir.ActivationFunctionType

D = 2048
TOK = 1024
NT = 8
ALPHA = 8.0 ** 0.25
LN_EPS = 1e-5
RH = 8
EPOCH = 12000
DLIMIT = 12000


class Buf:
    __slots__ = ("name", "w", "rd", "slot")

    def __init__(self, name):
        self.name = name
        self.w = None
        self.rd = {}
        self.slot = None


class KB:
    def __init__(self, nc, es):
        self.nc = nc
        self.es = es
        self.eng = {"pe": nc.tensor, "act": nc.scalar, "dve": nc.vector,
                    "pool": nc.gpsimd, "sp": nc.sync}
        self.sems = {e: [] for e in self.eng}
        self.cnt = {e: 0 for e in self.eng}
        self.seen = {e: {} for e in self.eng}
        self.slots = []
        self.free_slots_k = {}
        self.local_bufs = []
        self.persistent_mode = True
        self.nbuf = 0
        self.nwait = 0
        self.nsem = 0

    def _newsem(self, name):
        self.nsem += 1
        return self.es.enter_context(self.nc.semaphore(f"{name}_{self.nsem}"))

    def sb(self, es, name, shape, dt):
        self.nalloc = getattr(self, "nalloc", 0) + 1
        return es.enter_context(self.nc.sbuf_tensor(f"{name}_{self.nalloc}", list(shape), dt))

    def ps(self, es, name, shape, dt):
        self.nalloc = getattr(self, "nalloc", 0) + 1
        return es.enter_context(self.nc.psum_tensor(f"{name}_{self.nalloc}", list(shape), dt))

    def buf(self, name=None):
        self.nbuf += 1
        b = Buf(name or f"b{self.nbuf}")
        if not getattr(self, "persistent_mode", False):
            self.local_bufs.append(b)
        return b

    def bufs(self, n, name="b"):
        return [self.buf(f"{name}{i}") for i in range(n)]

    def _esem(self, e, n):
        k = (n - 1) // EPOCH
        while len(self.sems[e]) <= k:
            self.sems[e].append(self._newsem("s_" + e))
        return self.sems[e][k], n - k * EPOCH

    def _slot(self, b, q="sp"):
        if b.slot is None:
            kind = "sw" if q == "pool" else ("cc" if q == "cc" else "hw")
            fl = self.free_slots_k.setdefault(kind, [])
            if fl:
                s = fl.pop(0)
            elif len(self.slots) < 80:
                s = {"sem": self._newsem("d"), "val": 0, "id": len(self.slots), "kind": kind}
                self.slots.append(s)
            else:
                cand = [x for x in self.slots if x["kind"] == kind]
                self.nshare = getattr(self, "nshare", 0) + 1
                s = cand[self.nshare % len(cand)]
            b.slot = s
        return b.slot

    def _wait_tok(self, e, tok):
        if tok[0] == "e":
            key, val = tok[1], tok[2]
            if self.seen[e].get(key, 0) >= val:
                return
            sem, v = self._esem(tok[1], val)
            self.eng[e].wait_ge(sem, v)
        else:
            s = tok[1]
            key, val = ("d", s["id"], id(s["sem"])), s["val"]
            if self.seen[e].get(key, 0) >= val:
                return
            self.eng[e].wait_ge(s["sem"], val)
        self.seen[e][key] = val
        self.nwait += 1

    def _deps(self, e, reads, writes, is_dma=False):
        for b in reads:
            if b.w is not None:
                self._wait_tok(e, b.w)
        strict = e in ("act", "dve")
        for b in writes:
            if b.w is not None and (strict or not (b.w[0] == "e" and b.w[1] == e)) and not (is_dma and b.w[0] == "d"):
                self._wait_tok(e, b.w)
            for r in b.rd.values():
                if r[0] == "e" and r[1] == e and not strict:
                    continue
                self._wait_tok(e, r)

    def op(self, e, fn, reads=(), writes=()):
        self._deps(e, reads, writes)
        inst = fn(self.eng[e])
        self.cnt[e] += 1
        sem, _ = self._esem(e, self.cnt[e])
        inst.then_inc(sem, 1)
        t = ("e", e, self.cnt[e])
        for b in reads:
            b.rd[e] = t
        for b in writes:
            b.w = t
            b.rd = {}
        return inst

    def _slot_inc(self, q, s, amount):
        if s["val"] + amount > DLIMIT:
            self._wait_tok(q, ("d", s))
            s["sem"] = self._newsem("d")
            s["val"] = 0
        s["val"] += amount

    def dma(self, q, out_ap, in_ap, src, dst, **kw):
        reads = [src] if src is not None else []
        self._deps(q, reads, [dst], is_dma=True)
        s = self._slot(dst, q)
        self._slot_inc(q, s, 16)
        inst = self.eng[q].dma_start(out=out_ap, in_=in_ap, **kw)
        inst.then_inc(s["sem"], 16)
        t = ("d", s)
        if src is not None:
            src.rd[("d", s["id"])] = t
        dst.w = t
        dst.rd = {}
        return inst

    def allgather(self, in_ap, out_ap, src, dst, groups):
        self._deps("pool", [src], [dst])
        s = self._slot(dst, "cc")
        self._slot_inc("pool", s, 1)
        inst = self.nc.gpsimd.collective_compute("AllGather", ALU.bypass, replica_groups=groups,
                                                 ins=[in_ap.opt()], outs=[out_ap.opt()])
        inst.then_inc(s["sem"], 1)
        t = ("d", s)
        src.rd[("d", s["id"])] = t
        dst.w = t
        dst.rd = {}

    def barrier(self):
        for e in self.eng:
            for x in ("pe", "act", "dve"):
                if self.cnt[x] > 0:
                    self._wait_tok(e, ("e", x, self.cnt[x]))
            for sl in self.slots:
                if sl["val"] > 0:
                    self._wait_tok(e, ("d", sl))
        for b in self.local_bufs:
            if b.slot is not None:
                fl = self.free_slots_k.setdefault(b.slot["kind"], [])
                if b.slot not in fl:
                    fl.append(b.slot)
            b.slot = None
        self.local_bufs = []

    def wait_all(self, e, bufs):
        for b in bufs:
            if b.w is not None:
                self._wait_tok(e, b.w)


def bc(ap, dim, n):
    l = [list(x) for x in ap.ap]
    l.insert(dim, [0, n])
    return bass.AP(ap.tensor, ap.offset, l)


def _ret_tables():
    h = np.arange(RH, dtype=np.float64)
    lg = np.log1p(-np.exp2(-5.0 - h))
    c = np.arange(128, dtype=np.float64)
    rel = c[None, :] - c[:, None]
    decayT = np.where(rel[None] >= 0, np.exp(lg[:, None, None] * np.maximum(rel[None], 0)), 0.0) / 16.0
    decayT = np.ascontiguousarray(decayT.transpose(1, 0, 2)).astype(np.float32)
    dq = np.exp(lg[:, None] * (c[None, :] + 1.0))
    dq = np.broadcast_to(dq[None], (128, RH, 128)).astype(np.float32).copy()
    dkc = (np.exp(lg[None, :] * (127.0 - c[:, None])) / 16.0).astype(np.float32)
    t = np.arange(NT, dtype=np.float64)
    pos = t[None, None, :] * 128 + c[:, None, None]
    dkf = (np.exp(lg[None, :, None] * (1023.0 - pos)) / 16.0).astype(np.float32)
    g128 = np.exp(lg * 128.0)
    return lg, decayT, dq, dkc, dkf, g128


def _coef(lg, quarter):
    co = np.zeros((RH, 4), np.float64)
    for j in range(4):
        if j < quarter:
            co[:, j] = np.exp(lg * 1024.0 * (quarter - 1 - j))
    return np.broadcast_to(co[None], (128, RH, 4)).astype(np.float32).copy()


class Prog:
    pass


def build_program(n_layers=4, debug=False):
    nc = bass.Bass("TRN2", target_bir_lowering=False)
    P = Prog()
    dt_in = lambda name, shape, dt=F32: nc.dram_tensor(name, list(shape), dt, kind="ExternalInput").ap()
    dt_int = lambda name, shape, dt=F32: nc.dram_tensor(name, list(shape), dt, kind="Internal").ap()
    dt_dbg = lambda name, shape, dt=F32: nc.dram_tensor(name, list(shape), dt, kind=("ExternalOutput" if debug else "Internal")).ap()
    P.dbg_names = []

    x_in = dt_in("x", [TOK, D])
    pos_in = dt_in("pos", [1, TOK], I32)
    ident_in = dt_in("ident", [128, 128], BF16)
    ident32_in = dt_in("ident32", [128, 128], F32)
    invf_in = dt_in("invf", [128, 1])
    decayT_in = dt_in("decayT", [128, RH, 128])
    dq_in = dt_in("dq", [128, RH, 128])
    dkc_in = dt_in("dkc", [128, RH])
    dkf_in = dt_in("dkf", [128, RH, NT])
    coef_in = dt_in("coef", [128, RH, 4])
    w_in = [dt_in(f"rwin{l}", [D, 12288]) for l in range(min(2, n_layers))]
    w_out = [dt_in(f"rwout{l}", [4096, D]) for l in range(min(2, n_layers))]
    ln_g = [dt_in(f"lng{l}", [1, D]) for l in range(n_layers)]
    ln_b = [dt_in(f"lnb{l}", [1, D]) for l in range(n_layers)]
    y_out = nc.dram_tensor("y", [TOK, D], F32, kind="ExternalOutput").ap()

    P.dbg_names.append("qT_d")
    qT_d = dt_dbg("qT_d", [RH * 2 * 128, TOK], BF16)
    P.dbg_names.append("kT_d")
    kT_d = dt_dbg("kT_d", [RH * 2 * 128, TOK], BF16)
    P.dbg_names.append("v_d")
    v_d = dt_dbg("v_d", [TOK, 4096], BF16)
    P.dbg_names.append("z_d")
    z_d = dt_dbg("z_d", [TOK, 4096], F32)
    sloc_d = [dt_int(f"sloc_d{c}", [1024, 512], BF16) for c in range(2)]
    sloc_all = [dt_int(f"sloc_all{c}", [4 * 1024, 512], BF16) for c in range(2)]
    xres_d = [dt_int(f"xres{i}", [TOK, D], F32) for i in range(2)]
    P.dbg_names.append("vres_d")
    vres_d = dt_dbg("vres_d", [TOK, D], F32)

    NSA = n_layers > 2
    if NSA:
        nsa_w_kv = dt_in("nsa_w_kv", [D, 3840])
        nsa_pe_k = dt_in("nsa_pe_k", [32, 192]); nsa_pe_v = dt_in("nsa_pe_v", [32, 128])
        nsa_w_ck1 = dt_in("nsa_w_ck1", [6144, 256]); nsa_w_ck2 = dt_in("nsa_w_ck2", [256, 192])
        nsa_w_cv1 = dt_in("nsa_w_cv1", [4096, 256]); nsa_w_cv2 = dt_in("nsa_w_cv2", [256, 128])
        nsa_w_in = [dt_in(f"nwin{l}", [D, 9264]) for l in range(2, n_layers)]
        nsa_w_out = [dt_in(f"nwout{l}", [D, D]) for l in range(2, n_layers)]
        cmask_in = dt_in("cmask", [128, 2, TOK], BF16)
        addtab_in = dt_in("addtab", [128, NT, 64]); valid_in = dt_in("valid", [128, NT, 64])
        tabO_in = dt_in("tabO", [128, NT, 64]); tabL_in = dt_in("tabL", [128, NT, 64])
        tri_in = dt_in("tri", [128, 128], BF16); triU_in = dt_in("triU", [128, 128], BF16)
        hsel_in = dt_in("hsel", [128, 4])
        eall_in = dt_in("eall", [64, 4096], BF16); eown_in = dt_in("eown", [64, TOK], BF16)
        ovl_in = dt_in("ovl", [128, 2, 64], BF16)
        fm_d = [dt_int(f"fm_d{i}", [384, TOK], BF16) for i in range(6)]
        fm_all = [dt_int(f"fm_all{i}", [4 * 384, TOK], BF16) for i in range(6)]
        vcT_d = dt_int("vcT_d", [512, TOK], BF16); vcT_all = dt_int("vcT_all", [4 * 512, TOK], BF16)
        vs_d = dt_int("vs_d", [TOK, 512], BF16); vs_all = dt_int("vs_all", [4 * TOK, 512], BF16)
        vw_d = dt_int("vw_d", [TOK, 512], BF16); vw_all = dt_int("vw_all", [4 * TOK, 512], BF16)
        qn_d = dt_int("qn_d", [16 * 192, TOK], BF16)
        G_d = [dt_int(f"G_d{i}", [TOK, D], F32) for i in range(3)]
    fence_d = dt_int("fence_d", [16, 64], F32); fence_all = dt_int("fence_all", [64, 64], F32)
    g128 = _ret_tables()[5]
    GROUPS = [[0, 1, 2, 3], [4, 5, 6, 7]]

    with ExitStack() as es:
        kb = KB(nc, es)
        ident = kb.sb(es, "ident", [128, 128], BF16); b_ident = kb.buf("ident")
        xT = kb.sb(es, "xT", [128, 16, TOK], BF16); b_xT = kb.buf("xT")
        cosF = kb.sb(es, "cosF", [128, TOK], F32); sinF = kb.sb(es, "sinF", [128, TOK], F32)
        b_cs = kb.buf("cossin")
        b_qT_d = kb.buf("qT_d"); b_kT_d = kb.buf("kT_d"); b_v_d = kb.buf("v_d"); b_z_d = kb.buf("z_d")
        b_sloc_d = kb.bufs(2, "sloc_d"); b_sloc_all = kb.bufs(2, "sloc_all")
        b_xres = [kb.buf("xres0"), kb.buf("xres1")]; b_vres = kb.buf("vres"); b_y = kb.buf("y")


        if NSA:
            kcA = kb.sb(es, "kcA", [128, 4, 256], BF16); kcB = kb.sb(es, "kcB", [64, 4, 256], BF16)
            vcx = kb.sb(es, "vcx", [128, 2, 4, 193], BF16); b_kc = kb.buf("kc")
            b_fm_d = kb.bufs(6, "fm_d"); b_fm_all = kb.bufs(6, "fm_all")
            b_vcT_d = kb.buf("vcT_d"); b_vcT_all = kb.buf("vcT_all")
            b_vs_d = kb.buf("vs_d"); b_vs_all = kb.buf("vs_all"); b_vw_d = kb.buf("vw_d"); b_vw_all = kb.buf("vw_all")
            b_qn_d = kb.buf("qn_d"); b_G_d = kb.buf("G_d")

        kb.dma("sp", ident[:], ident_in[:, :], None, b_ident)
        b_fence_d = kb.buf("fence_d"); b_fence_all = kb.buf("fence_all")
        fz = kb.sb(es, "fence_z", [16, 64], F32); b_fz = kb.buf("fence_z")
        kb.op("dve", lambda e: e.memset(fz[:], 0.0), [], [b_fz])
        kb.dma("sp", fence_d[:, :], fz[:], b_fz, b_fence_d)

        def cc_fence():
            kb.allgather(fence_d[:, :], fence_all[:, :], b_fence_d, b_fence_all, GROUPS)
            kb.barrier()

        def make_xT(ses, name):
            xb = [kb.sb(ses, f"{name}_xb{i}", [128, D], BF16) for i in range(2)]
            b_xb = kb.bufs(2, name + "_xb")
            pT = [kb.ps(ses, f"{name}_pT{i}", [128, 1024], BF16) for i in range(2)]
            b_pT = kb.bufs(2, name + "_pT")

            def run(t, src_ap, b_src):
                s = t % 2
                kb.op("act", lambda e: e.copy(xb[s][:], src_ap), [b_src], [b_xb[s]])
                for half in range(2):
                    for j in range(8):
                        c = half * 8 + j
                        kb.op("pe", lambda e: e.transpose(pT[half][:, j * 128:(j + 1) * 128],
                                                          xb[s][:, c * 128:(c + 1) * 128], ident[:]),
                              [b_xb[s], b_ident], [b_pT[half]])
                    kb.op("dve", lambda e: e.tensor_copy(
                        xT[:, half * 8:(half + 1) * 8, t * 128:(t + 1) * 128],
                        pT[half][:].rearrange("p (c t) -> p c t", c=8)), [b_pT[half]], [b_xT])
            return run

        kb.persistent_mode = False
        with ExitStack() as ses:
            xt32 = [kb.sb(ses, f"s0_x{i}", [128, D], F32) for i in range(2)]
            b_xt32 = kb.bufs(2, "s0_x")
            mk = make_xT(ses, "s0")
            for t in range(NT):
                s = t % 2
                kb.dma("sp", xt32[s][:], x_in[t * 128:(t + 1) * 128, :], None, b_xt32[s])
                mk(t, xt32[s][:], b_xt32[s])
            posi = kb.sb(ses, "s0_posi", [128, TOK], I32); b_posi = kb.buf("posi")
            invf = kb.sb(ses, "s0_invf", [128, 1], F32); b_invf = kb.buf("invf")
            ang = kb.sb(ses, "s0_ang", [128, TOK], F32); b_ang = kb.buf("ang")
            tmp = kb.sb(ses, "s0_tmp", [128, TOK], F32); b_tmp = kb.buf("tmp")
            kf = kb.sb(ses, "s0_kf", [128, TOK], F32); b_kf = kb.buf("kf")
            ki = kb.sb(ses, "s0_ki", [128, TOK], I32); b_ki = kb.buf("ki")
            kb.dma("sp", posi[:], pos_in.partition_broadcast(128).rearrange("p a t -> p (a t)"), None, b_posi)
            kb.dma("sp", invf[:], invf_in[:, :], None, b_invf)
            kb.op("dve", lambda e: e.tensor_copy(ang[:], posi[:]), [b_posi], [b_ang])
            kb.op("dve", lambda e: e.tensor_scalar(ang[:], ang[:], invf[:, 0:1], None, ALU.mult), [b_ang, b_invf], [b_ang])
            TWO_PI = 2.0 * float(np.pi)
            for which, dst in ((0, sinF), (1, cosF)):
                shift = 0.0 if which == 0 else 0.5 * float(np.pi)
                kb.op("dve", lambda e: e.tensor_scalar(tmp[:], ang[:], shift, None, ALU.add), [b_ang], [b_tmp])
                kb.op("dve", lambda e: e.tensor_scalar(kf[:], tmp[:], 1.0 / TWO_PI, None, ALU.mult), [b_tmp], [b_kf])
                kb.op("dve", lambda e: e.tensor_copy(ki[:], kf[:]), [b_kf], [b_ki])
                kb.op("dve", lambda e: e.tensor_copy(kf[:], ki[:]), [b_ki], [b_kf])
                kb.op("dve", lambda e: e.scalar_tensor_tensor(tmp[:], kf[:], -TWO_PI, tmp[:], ALU.mult, ALU.add), [b_kf, b_tmp], [b_tmp])
                kb.op("dve", lambda e: e.tensor_scalar(kf[:], tmp[:], TWO_PI, -TWO_PI, ALU.is_ge, ALU.mult), [b_tmp], [b_kf])
                kb.op("dve", lambda e: e.tensor_tensor(tmp[:], tmp[:], kf[:], ALU.add), [b_tmp, b_kf], [b_tmp])
                kb.op("dve", lambda e: e.tensor_scalar(kf[:], tmp[:], 0.0, TWO_PI, ALU.is_lt, ALU.mult), [b_tmp], [b_kf])
                kb.op("dve", lambda e: e.tensor_tensor(tmp[:], tmp[:], kf[:], ALU.add), [b_tmp, b_kf], [b_tmp])
                kb.op("dve", lambda e: e.tensor_scalar(tmp[:], tmp[:], -1.0, float(np.pi), ALU.mult, ALU.add), [b_tmp], [b_tmp])
                kb.op("dve", lambda e: e.tensor_scalar(tmp[:], tmp[:], 3.1415925, -3.1415925, ALU.min, ALU.max), [b_tmp], [b_tmp])
                kb.op("act", lambda e: e.activation(dst[:], tmp[:], AF.Sin), [b_tmp], [b_cs])
            kb.barrier()
            if debug:
                dbg_cs = nc.dram_tensor("dbg_cs", [128, 2, TOK], F32, kind="ExternalOutput").ap()
                P.dbg_names.append("dbg_cs")
                b_dbg = kb.buf("dbg")
                kb.dma("sp", dbg_cs[:, 0, :], cosF[:], b_cs, b_dbg)
                kb.dma("sp", dbg_cs[:, 1, :], sinF[:], b_cs, b_dbg)
                P.b_dbg = b_dbg

        def retention_layer(l):
            xres_in_ap = x_in if l == 0 else xres_d[(l - 1) % 2]
            b_xres_in = None if l == 0 else b_xres[(l - 1) % 2]
            xres_out_ap = xres_d[l % 2]; b_xres_out = b_xres[l % 2]
            W = w_in[l].rearrange("(c p) n -> p c n", p=128)
            with ExitStack() as ses:
                wt = [kb.sb(ses, f"ra_w{i}", [128, 16, 512], BF16) for i in range(3)]
                b_wt = kb.bufs(3, "ra_w")
                dkf = kb.sb(ses, "ra_dkf", [128, RH, NT], F32); b_dkf = kb.buf("dkf")
                kb.dma("sp", dkf[:], dkf_in[:, :, :], None, b_dkf)
                qk = [kb.sb(ses, f"ra_qk{i}", [128, 2, TOK], BF16) for i in range(2)]
                b_qk = kb.bufs(2, "ra_qk")
                tcs = [kb.sb(ses, f"ra_tc{i}", [128, 2, 512], F32) for i in range(2)]
                b_tcs = kb.bufs(2, "ra_tc")
                ktm = kb.sb(ses, "ra_ktm", [128, NT, 256], BF16); b_ktm = kb.buf("ktm")
                vh = [kb.sb(ses, f"ra_v{i}", [128, NT, 512], BF16) for i in range(2)]; b_vh = kb.bufs(2, "ra_v")
                zt = [kb.sb(ses, f"ra_z{i}", [128, 512], F32) for i in range(8)]; b_zt = kb.bufs(8, "ra_z")
                sl = kb.sb(ses, "ra_sl", [128, 2, 512], BF16); b_sl = kb.buf("sl")
                pQK = [kb.ps(ses, f"ra_pQK{i}", [128, 2, 512], F32) for i in range(2)]; b_pQK = kb.bufs(2, "pQK")
                pV = [kb.ps(ses, f"ra_pV{i}", [128, 512], F32) for i in range(2)]; b_pV = kb.bufs(2, "pV")
                pT = kb.ps(ses, "ra_pT", [128, 1024], BF16); b_pT = kb.buf("ra_pT")
                pS = kb.ps(ses, "ra_pS", [128, 512], F32); b_pS = kb.buf("ra_pS")
                wi = 0
                pvi = 0
                zi = 0

                def load_w(slot, col0, ncols, dst0=0):
                    kb.dma("pool", wt[slot][:, :, dst0:dst0 + ncols], W[:, :, col0:col0 + ncols], None, b_wt[slot])

                def wcols(h, kind):
                    if kind == 0:
                        return [(h * 256, 256, 0), (2048 + h * 256, 256, 256)]
                    if kind == 1:
                        return [(4096 + h * 512, 512, 0)]
                    return [(8192 + h * 512, 512, 0)]
                seq = [(h, kind) for h in range(RH) for kind in range(3)]

                def issue_w(i):
                    if i < len(seq):
                        h, kind = seq[i]
                        for (c0, n, d0) in wcols(h, kind):
                            load_w(i % 3, c0, n, d0)
                issue_w(0); issue_w(1)
                for i, (h, kind) in enumerate(seq):
                    issue_w(i + 2)
                    ws = i % 3
                    if kind == 0:
                        for which in range(2):
                            dst = qk[which]
                            for tg in range(2):
                                pp = pQK[(which * 2 + tg) % 2]; b_pp = b_pQK[(which * 2 + tg) % 2]
                                for half in range(2):
                                    for c in range(16):
                                        kb.op("pe", lambda e: e.matmul(
                                            pp[:, half, :], wt[ws][:, c, which * 256 + half * 128: which * 256 + half * 128 + 128],
                                            xT[:, c, tg * 512:(tg + 1) * 512], start=(c == 0), stop=(c == 15)),
                                            [b_wt[ws], b_xT], [b_pp])
                                cs_c = bc(cosF[:, tg * 512:(tg + 1) * 512], 1, 2)
                                cs_s = bc(sinF[:, tg * 512:(tg + 1) * 512], 1, 2)
                                kb.op("dve", lambda e: e.tensor_tensor(tcs[0][:], pp[:], cs_c, ALU.mult), [b_pp, b_cs], [b_tcs[0]])
                                kb.op("dve", lambda e: e.tensor_tensor(tcs[1][:], pp[:], cs_s, ALU.mult), [b_pp, b_cs], [b_tcs[1]])
                                kb.op("dve", lambda e: e.tensor_tensor(dst[:, 0, tg * 512:(tg + 1) * 512], tcs[0][:, 0, :], tcs[1][:, 1, :], ALU.subtract),
                                      [b_tcs[0], b_tcs[1]], [b_qk[which]])
                                kb.op("dve", lambda e: e.tensor_tensor(dst[:, 1, tg * 512:(tg + 1) * 512], tcs[1][:, 0, :], tcs[0][:, 1, :], ALU.add),
                                      [b_tcs[0], b_tcs[1]], [b_qk[which]])
                            dd = qT_d if which == 0 else kT_d
                            kb.dma("sp", dd[h * 256:(h + 1) * 256, :].rearrange("(a p) t -> p a t", p=128), dst[:], b_qk[which],
                                   b_qT_d if which == 0 else b_kT_d)
                        for tgrp in range(2):
                            for tt in range(4):
                                t = tgrp * 4 + tt
                                for half in range(2):
                                    kb.op("pe", lambda e: e.transpose(pT[:, (tt * 2 + half) * 128:(tt * 2 + half + 1) * 128],
                                                                      qk[1][:, half, t * 128:(t + 1) * 128], ident[:]),
                                          [b_qk[1], b_ident], [b_pT])
                            for tt in range(4):
                                t = tgrp * 4 + tt
                                kb.op("act", lambda e: e.mul(ktm[:, t, :], pT[:, tt * 256:(tt + 1) * 256], dkf[:, h, t:t + 1]),
                                      [b_pT, b_dkf], [b_ktm])
                    elif kind == 1:
                        vs = h % 2
                        for t in range(NT):
                            pp = pV[pvi % 2]; b_pp = b_pV[pvi % 2]; pvi += 1
                            for c in range(16):
                                kb.op("pe", lambda e: e.matmul(pp[:], xT[:, c, t * 128:(t + 1) * 128], wt[ws][:, c, :],
                                                               start=(c == 0), stop=(c == 15)), [b_xT, b_wt[ws]], [b_pp])
                            kb.op("act", lambda e: e.copy(vh[vs][:, t, :], pp[:]), [b_pp], [b_vh[vs]])
                        kb.dma("sp", v_d[:, h * 512:(h + 1) * 512].rearrange("(t p) e -> p t e", p=128), vh[vs][:], b_vh[vs], b_v_d)
                        for half in range(2):
                            for t in range(NT):
                                kb.op("pe", lambda e: e.matmul(pS[:], ktm[:, t, half * 128:(half + 1) * 128], vh[vs][:, t, :],
                                                               start=(t == 0), stop=(t == NT - 1)), [b_ktm, b_vh[vs]], [b_pS])
                            kb.op("act", lambda e: e.copy(sl[:, half, :], pS[:]), [b_pS], [b_sl])
                        kb.dma("sp", sloc_d[h // 4][(h % 4) * 256:(h % 4 + 1) * 256, :].rearrange("(a p) e -> p a e", p=128), sl[:], b_sl, b_sloc_d[h // 4])
                    else:
                        for t in range(NT):
                            pp = pV[pvi % 2]; b_pp = b_pV[pvi % 2]; pvi += 1
                            for c in range(16):
                                kb.op("pe", lambda e: e.matmul(pp[:], xT[:, c, t * 128:(t + 1) * 128], wt[ws][:, c, :],
                                                               start=(c == 0), stop=(c == 15)), [b_xT, b_wt[ws]], [b_pp])
                            zs = zi % 8; zi += 1
                            kb.op("act", lambda e: e.activation(zt[zs][:], pp[:], AF.Silu), [b_pp], [b_zt[zs]])
                            kb.dma("sp", z_d[t * 128:(t + 1) * 128, h * 512:(h + 1) * 512], zt[zs][:], b_zt[zs], b_z_d)
            kb.barrier()
            import os
            for c in range(2):
                if os.environ.get("NO_CC"):
                    for j in range(4):
                        kb.dma("sp", sloc_all[c][j * 1024:(j + 1) * 1024, :], sloc_d[c][:, :], b_sloc_d[c], b_sloc_all[c])
                else:
                    kb.allgather(sloc_d[c][:, :], sloc_all[c][:, :], b_sloc_d[c], b_sloc_all[c], GROUPS)
            cc_fence()

            ses_o = ExitStack()
            ogT = kb.sb(ses_o, "rb_ogT", [128, 32, TOK], BF16); b_ogT = kb.buf("ogT")
            with ExitStack() as ses:
                decayT = kb.sb(ses, "rb_decayT", [128, RH, 128], F32)
                dq = kb.sb(ses, "rb_dq", [128, RH, 128], F32)
                dkc = kb.sb(ses, "rb_dkc", [128, RH], F32)
                coef = kb.sb(ses, "rb_coef", [128, RH, 4], F32)
                b_tab = kb.buf("rb_tab")
                kb.dma("sp", decayT[:], decayT_in[:, :, :], None, b_tab)
                kb.dma("sp", dq[:], dq_in[:, :, :], None, b_tab)
                kb.dma("sp", dkc[:], dkc_in[:, :], None, b_tab)
                kb.dma("sp", coef[:], coef_in[:, :, :], None, b_tab)
                qT = [kb.sb(ses, f"rb_qT{i}", [128, 2, TOK], BF16) for i in range(2)]; b_qT = kb.bufs(2, "rb_qT")
                kT = [kb.sb(ses, f"rb_kT{i}", [128, 2, TOK], BF16) for i in range(2)]; b_kT = kb.bufs(2, "rb_kT")
                vv = [kb.sb(ses, f"rb_v{i}", [128, NT, 512], BF16) for i in range(2)]; b_vv = kb.bufs(2, "rb_v")
                sj = [kb.sb(ses, f"rb_sj{i}", [128, 2, 512], BF16) for i in range(2)] * 2; b_sj = kb.bufs(2, "rb_sj") * 2
                qd = [kb.sb(ses, f"rb_qd{i}", [128, 2, TOK], BF16) for i in range(2)]; b_qd = kb.bufs(2, "rb_qd")
                ktm = [kb.sb(ses, f"rb_ktm{i}", [128, NT, 256], BF16) for i in range(2)]; b_ktm = kb.bufs(2, "rb_ktm")
                st = [kb.sb(ses, f"rb_st{i}", [128, 2, 512], F32) for i in range(2)]; b_st = kb.bufs(2, "rb_st")
                stb = [kb.sb(ses, f"rb_stb{i}", [128, 2, 512], BF16) for i in range(2)]; b_stb = kb.bufs(2, "rb_stb")
                sTa = [kb.sb(ses, f"rb_sTa{i}", [128, NT, 128], BF16) for i in range(2)]; b_sTa = kb.bufs(2, "rb_sTa")
                zz = [kb.sb(ses, f"rb_z{i}", [128, 512], F32) for i in range(4)]; b_zz = kb.bufs(4, "rb_z")
                oc = [kb.sb(ses, f"rb_oc{i}", [128, 512], F32) for i in range(2)]; b_oc = kb.bufs(2, "rb_oc")
                og = [kb.sb(ses, f"rb_og{i}", [128, 512], BF16) for i in range(4)]; b_og = kb.bufs(4, "rb_og")
                stats = [kb.sb(ses, f"rb_stats{i}", [128, 8], F32) for i in range(2)]; b_stats = kb.bufs(2, "rb_stats")
                mv = [kb.sb(ses, f"rb_mv{i}", [128, 4], F32) for i in range(2)]; b_mv = kb.bufs(2, "rb_mv")
                pUt = [kb.ps(ses, f"rb_pU{i}", [128, 1024], F32) for i in range(2)]; b_pUt = kb.bufs(2, "rb_pU")
                pOt = [kb.ps(ses, f"rb_pO{i}", [128, 512], F32) for i in range(2)]; b_pOt = kb.bufs(2, "rb_pO")
                pT = [kb.ps(ses, f"rb_pT{i}", [128, 1024], BF16) for i in range(2)]; b_pT = kb.bufs(2, "rb_pT")
                pO = pOt; b_pO = b_pOt
                pb = [pUt[1][:, 0:512], pUt[1][:, 512:1024]]; b_pb = [b_pUt[1], b_pUt[1]]
                zi = [0]

                def load_head(h, s):
                    kb.dma("sp", qT[s][:], qT_d[h * 256:(h + 1) * 256, :].rearrange("(a p) t -> p a t", p=128), b_qT_d, b_qT[s])
                    kb.dma("sp", kT[s][:], kT_d[h * 256:(h + 1) * 256, :].rearrange("(a p) t -> p a t", p=128), b_kT_d, b_kT[s])
                    kb.dma("sp", vv[s][:], v_d[:, h * 512:(h + 1) * 512].rearrange("(t p) e -> p t e", p=128), b_v_d, b_vv[s])

                def setup(h, s):
                    st2 = st[s][:].rearrange("p a e -> p (a e)")

                    def ldj(j):
                        r0 = j * 1024 + (h % 4) * 256
                        kb.dma("sp", sj[j][:], sloc_all[h // 4][r0:r0 + 256, :].rearrange("(a p) e -> p a e", p=128), b_sloc_all[h // 4], b_sj[j])
                    ldj(0); ldj(1)
                    kb.op("dve", lambda e: e.tensor_scalar(st2, sj[0][:].rearrange("p a e -> p (a e)"), coef[:, h, 0:1], None, ALU.mult),
                          [b_sj[0], b_tab], [b_st[s]])
                    ldj(2)
                    for j in range(1, 4):
                        kb.op("dve", lambda e: e.scalar_tensor_tensor(st2, sj[j][:].rearrange("p a e -> p (a e)"), coef[:, h, j:j + 1], st2,
                                                                      ALU.mult, ALU.add), [b_sj[j], b_tab, b_st[s]], [b_st[s]])
                        if j == 1:
                            ldj(3)
                    kb.op("act", lambda e: e.copy(stb[s][:], st[s][:]), [b_st[s]], [b_stb[s]])
                    kb.op("dve", lambda e: e.tensor_tensor(qd[s][:].rearrange("p a (c t) -> p a c t", t=128),
                                                           qT[s][:].rearrange("p a (c t) -> p a c t", t=128),
                                                           bc(bc(dq[:, h, :], 1, NT), 1, 2), ALU.mult), [b_qT[s], b_tab], [b_qd[s]])
                    for tgrp in range(2):
                        pp = pT[s]; b_pp = b_pT[s]
                        for tt in range(4):
                            t = tgrp * 4 + tt
                            for half in range(2):
                                kb.op("pe", lambda e: e.transpose(pp[:, (tt * 2 + half) * 128:(tt * 2 + half + 1) * 128],
                                                                  kT[s][:, half, t * 128:(t + 1) * 128], ident[:]),
                                      [b_kT[s], b_ident], [b_pp])
                        kb.op("act", lambda e: e.mul(ktm[s][:, tgrp * 4:(tgrp + 1) * 4, :].rearrange("p t d -> p (t d)"), pp[:], dkc[:, h:h + 1]),
                              [b_pp, b_tab], [b_ktm[s]])
                    for cg in range(2):
                        ps_ = pb[cg]; b_ps = b_pb[cg]
                        for cc in range(4):
                            c = cg * 4 + cc
                            tsl = slice(c * 128, (c + 1) * 128)
                            for half in range(2):
                                kb.op("pe", lambda e: e.matmul(ps_[:, cc * 128:(cc + 1) * 128], kT[s][:, half, tsl], qT[s][:, half, tsl],
                                                               start=(half == 0 and cc == 0), stop=(half == 1)), [b_kT[s], b_qT[s]], [b_ps])
                        kb.op("dve", lambda e: e.tensor_tensor(sTa[s][:, cg * 4:(cg + 1) * 4, :], ps_.rearrange("p (c t) -> p c t", t=128),
                                                               bc(decayT[:, h, :], 1, 4), ALU.mult), [b_ps, b_tab], [b_sTa[s]])

                def step(h, s, c):
                    tsl = slice(c * 128, (c + 1) * 128)
                    kb.op("pe", lambda e: e.matmul(pO[s][:], sTa[s][:, c, :], vv[s][:, c, :], start=True, stop=False),
                          [b_sTa[s], b_vv[s]], [b_pO[s]])
                    for half in range(2):
                        kb.op("pe", lambda e: e.matmul(pO[s][:], qd[s][:, half, tsl], stb[s][:, half, :], start=False, stop=(half == 1)),
                              [b_qd[s], b_stb[s]], [b_pO[s]])
                    if c < NT - 1:
                        for half in range(2):
                            kb.op("pe", lambda e: e.matmul(pUt[s][:, half * 512:(half + 1) * 512], ktm[s][:, c, half * 128:(half + 1) * 128], vv[s][:, c, :],
                                                           start=True, stop=True), [b_ktm[s], b_vv[s]], [b_pUt[s]])
                        stf = st[s][:].rearrange("p a e -> p (a e)")
                        kb.op("dve", lambda e: e.scalar_tensor_tensor(stf, stf, float(g128[h]), pUt[s][:], ALU.mult, ALU.add),
                              [b_st[s], b_pUt[s]], [b_st[s]])
                        kb.op("act", lambda e: e.copy(stb[s][:], st[s][:]), [b_st[s]], [b_stb[s]])

                zmap = {}

                def zload(h, s, c):
                    z6 = zi[0] % 4; zi[0] += 1
                    kb.dma("sp", zz[z6][:], z_d[c * 128:(c + 1) * 128, h * 512:(h + 1) * 512], b_z_d, b_zz[z6])
                    zmap[(s, c)] = z6

                mvb = kb.sb(ses, "rb_mvb", [128, 2, 4], F32); b_mvb = kb.buf("rb_mvb")

                def norm_stats(h, s, c):
                    kb.op("act", lambda e: e.copy(oc[s][:], pO[s][:]), [b_pO[s]], [b_oc[s]])
                    kb.op("dve", lambda e: e.bn_stats(stats[s][:, 0:6], oc[s][:]), [b_oc[s]], [b_stats[s]])
                    kb.op("dve", lambda e: e.bn_aggr(mvb[:, s, 0:2], stats[s][:, 0:6]), [b_stats[s]], [b_mvb])

                def norm_rstd():
                    kb.op("dve", lambda e: e.tensor_scalar(mvb[:, :, 2:3], mvb[:, :, 1:2], LN_EPS, None, ALU.add), [b_mvb], [b_mvb])
                    kb.op("act", lambda e: e.sqrt(mvb[:, :, 3:4], mvb[:, :, 2:3]), [b_mvb], [b_mvb])
                    kb.op("dve", lambda e: e.reciprocal(mvb[:, :, 2:3], mvb[:, :, 3:4]), [b_mvb], [b_mvb])
                    kb.op("dve", lambda e: e.scalar_tensor_tensor(mvb[:, :, 3:4], mvb[:, :, 0:1], -1.0, mvb[:, :, 2:3], ALU.mult, ALU.mult), [b_mvb], [b_mvb])

                def norm_apply(h, s, c):
                    z6 = zmap[(s, c)]
                    kb.op("act", lambda e: e.activation(oc[s][:], oc[s][:], AF.Identity, bias=mvb[:, s, 3:4], scale=mvb[:, s, 2:3]),
                          [b_oc[s], b_mvb], [b_oc[s]])
                    oi = (c % 2) * 2 + s
                    kb.op("dve", lambda e: e.tensor_tensor(og[oi][:], oc[s][:], zz[z6][:], ALU.mult), [b_oc[s], b_zz[z6]], [b_og[oi]])

                def post(h, s, c):
                    tsl = slice(c * 128, (c + 1) * 128)
                    oi = (c % 2) * 2 + s
                    pp = pT[s]; b_pp = b_pT[s]
                    for j in range(4):
                        kb.op("pe", lambda e: e.transpose(pp[:, j * 128:(j + 1) * 128], og[oi][:, j * 128:(j + 1) * 128], ident[:]),
                              [b_og[oi], b_ident], [b_pp])
                    kb.op("act", lambda e: e.copy(ogT[:, h * 4:(h + 1) * 4, tsl], pp[:, 0:512].rearrange("p (j t) -> p j t", j=4)),
                          [b_pp], [b_ogT])

                load_head(0, 0); load_head(1, 1)
                for hp in range(RH // 2):
                    ha, hb = 2 * hp, 2 * hp + 1
                    setup(ha, 0); setup(hb, 1)
                    zload(ha, 0, 0); zload(hb, 1, 0)
                    for c in range(NT):
                        if c + 1 < NT:
                            zload(ha, 0, c + 1); zload(hb, 1, c + 1)
                        step(ha, 0, c); step(hb, 1, c)
                        if c > 0:
                            post(ha, 0, c - 1); post(hb, 1, c - 1)
                        norm_stats(ha, 0, c); norm_stats(hb, 1, c)
                        norm_rstd()
                        norm_apply(ha, 0, c); norm_apply(hb, 1, c)
                    if hp + 1 < RH // 2:
                        load_head(ha + 2, 0); load_head(hb + 2, 1)
                    post(ha, 0, NT - 1); post(hb, 1, NT - 1)

            kb.barrier()
            with ExitStack() as ses:
                layer_epilogue(l, ses, ogT, b_ogT, 32, w_out[l], xres_in_ap, b_xres_in, xres_out_ap, b_xres_out)
            kb.barrier()
            ses_o.close()

        def layer_epilogue(l, ses, ogT, b_ogT, nk, wo_ap, xres_in_ap, b_xres_in, xres_out_ap, b_xres_out):
            last = (l == n_layers - 1)
            Wo = wo_ap.rearrange("(c p) n -> p c n", p=128)
            NW = 256 if nk > 16 else 512
            nwo = 2
            wo = [kb.sb(ses, f"rc_wo{i}", [128, nk, NW], BF16) for i in range(nwo)]; b_wo = kb.bufs(nwo, "rc_wo")
            xr = [kb.sb(ses, f"rc_xr{i}", [128, 512], F32) for i in range(4)]; b_xr = kb.bufs(4, "rc_xr")
            vt = [kb.sb(ses, f"rc_vt{i}", [128, 512], F32) for i in range(4)]; b_vt = kb.bufs(4, "rc_vt")
            stats = kb.sb(ses, "rc_stats", [128, NT, 8, 6], F32); b_stats = kb.buf("rc_stats")
            mv = kb.sb(ses, "rc_mv", [128, NT, 4], F32); b_mv = kb.buf("rc_mv")
            G = kb.sb(ses, "rc_G", [128, D], F32); Bt = kb.sb(ses, "rc_B", [128, D], F32); b_gb = kb.buf("rc_gb")
            vrow = [kb.sb(ses, f"rc_vrow{i}", [128, D], F32) for i in range(2)]; b_vrow = kb.bufs(2, "rc_vrow")
            pY = [kb.ps(ses, f"rc_pY{i}", [128, 512], F32) for i in range(4)]; b_pY = kb.bufs(4, "rc_pY")
            mk = make_xT(ses, "rc")
            kb.dma("sp", G[:], ln_g[l].partition_broadcast(128).rearrange("p a d -> p (a d)"), None, b_gb)
            kb.dma("sp", Bt[:], ln_b[l].partition_broadcast(128).rearrange("p a d -> p (a d)"), None, b_gb)
            nct = D // NW
            kb.dma("pool", wo[0][:], Wo[:, :, 0:NW], None, b_wo[0])
            order = [(n, t) for n in range(nct) for t in range(NT)]

            def load_xr(i):
                if i < len(order):
                    n, t = order[i]
                    kb.dma("sp", xr[i % 4][:, 0:NW], xres_in_ap[t * 128:(t + 1) * 128, n * NW:(n + 1) * NW], b_xres_in, b_xr[i % 4])
            load_xr(0); load_xr(1)
            for i, (n, t) in enumerate(order):
                if t == 0 and n + 1 < nct:
                    kb.dma("pool", wo[(n + 1) % 2][:], Wo[:, :, (n + 1) * NW:(n + 2) * NW], None, b_wo[(n + 1) % 2])
                ws = n % 2
                pi = i % 4
                s3 = i % 4
                load_xr(i + 2)
                for k in range(nk):
                    kb.op("pe", lambda e: e.matmul(pY[pi][:, 0:NW], ogT[:, k, t * 128:(t + 1) * 128], wo[ws][:, k, :],
                                                   start=(k == 0), stop=(k == nk - 1)), [b_ogT, b_wo[ws]], [b_pY[pi]])
                kb.op("dve", lambda e: e.scalar_tensor_tensor(vt[s3][:, 0:NW], xr[s3][:, 0:NW], float(ALPHA), pY[pi][:, 0:NW], ALU.mult, ALU.add),
                      [b_xr[s3], b_pY[pi]], [b_vt[s3]])
                kb.op("dve", lambda e: e.bn_stats(stats[:, t, n, :], vt[s3][:, 0:NW]), [b_vt[s3]], [b_stats])
                kb.dma("sp", vres_d[t * 128:(t + 1) * 128, n * NW:(n + 1) * NW], vt[s3][:, 0:NW], b_vt[s3], b_vres)
            for t in range(NT):
                kb.op("dve", lambda e: e.bn_aggr(mv[:, t, 0:2], stats[:, t, 0:D // NW, :].rearrange("p n s -> p (n s)")), [b_stats], [b_mv])
            kb.op("dve", lambda e: e.tensor_scalar(mv[:, :, 2:3], mv[:, :, 1:2], LN_EPS, None, ALU.add), [b_mv], [b_mv])
            kb.op("act", lambda e: e.sqrt(mv[:, :, 3:4], mv[:, :, 2:3]), [b_mv], [b_mv])
            kb.op("dve", lambda e: e.reciprocal(mv[:, :, 2:3], mv[:, :, 3:4]), [b_mv], [b_mv])
            kb.op("dve", lambda e: e.scalar_tensor_tensor(mv[:, :, 3:4], mv[:, :, 0:1], -1.0, mv[:, :, 2:3], ALU.mult, ALU.mult), [b_mv], [b_mv])
            for t in range(NT):
                s = t % 2
                kb.dma("sp", vrow[s][:], vres_d[t * 128:(t + 1) * 128, :], b_vres, b_vrow[s])
                kb.op("act", lambda e: e.activation(vrow[s][:], vrow[s][:], AF.Identity, bias=mv[:, t, 3:4], scale=mv[:, t, 2:3]),
                      [b_vrow[s], b_mv], [b_vrow[s]])
                kb.op("dve", lambda e: e.tensor_tensor(vrow[s][:], vrow[s][:], G[:], ALU.mult), [b_vrow[s], b_gb], [b_vrow[s]])
                kb.op("dve", lambda e: e.tensor_tensor(vrow[s][:], vrow[s][:], Bt[:], ALU.add), [b_vrow[s], b_gb], [b_vrow[s]])
                if last:
                    kb.dma("sp", y_out[t * 128:(t + 1) * 128, :], vrow[s][:], b_vrow[s], b_y)
                else:
                    kb.dma("sp", xres_out_ap[t * 128:(t + 1) * 128, :], vrow[s][:], b_vrow[s], b_xres_out)
                    mk(t, vrow[s][:], b_vrow[s])

        KV_OFF = dict(kcmp=0, vcmp=768, ksel=1280, vsel=2048, kwin=2560, vwin=3328)
        QSCALE = 192.0 ** -0.5
        BIG = 16384.0

        def fm_gemm(ses_bufs, wt_ap, b_wt, chunks, dst_ap, b_dst, row0):
            stage, b_stage, pF, b_pF, cnt = ses_bufs
            r = row0
            for (off, nr) in chunks:
                si = cnt[0] % 2; cnt[0] += 1
                for tg in range(2):
                    pi = cnt[1] % 2; cnt[1] += 1
                    for c in range(16):
                        kb.op("pe", lambda e: e.matmul(pF[pi][0:nr, :], wt_ap[:, c, off:off + nr], xT[:, c, tg * 512:(tg + 1) * 512],
                                                       start=(c == 0), stop=(c == 15)), [b_wt, b_xT], [b_pF[pi]])
                    kb.op("act", lambda e: e.copy(stage[si][0:nr, tg * 512:(tg + 1) * 512], pF[pi][0:nr, :]), [b_pF[pi]], [b_stage[si]])
                kb.dma("sp", dst_ap[r:r + nr, :], stage[si][0:nr, :], b_stage[si], b_dst)
                r += nr

        def nsa_kv():
            Wkv = nsa_w_kv.rearrange("(c p) n -> p c n", p=128)
            with ExitStack() as ses:
                wt = [kb.sb(ses, f"kv_w{i}", [128, 16, 512], BF16) for i in range(3)]; b_wt = kb.bufs(3, "kv_w")
                stage = [kb.sb(ses, f"kv_st{i}", [128, TOK], BF16) for i in range(2)]; b_stage = kb.bufs(2, "kv_st")
                vst = [kb.sb(ses, f"kv_vst{i}", [128, 512], BF16) for i in range(2)]; b_vst = kb.bufs(2, "kv_vst")
                pF = [kb.ps(ses, f"kv_pF{i}", [128, 512], F32) for i in range(2)]; b_pF = kb.bufs(2, "kv_pF")
                pV = [kb.ps(ses, f"kv_pV{i}", [128, 512], F32) for i in range(2)]; b_pV = kb.bufs(2, "kv_pV")
                fb = (stage, b_stage, pF, b_pF, [0, 0])
                tiles = [("fm", KV_OFF["kcmp"], 384, 0), ("fm", KV_OFF["kcmp"] + 384, 384, 1), ("vc", KV_OFF["vcmp"], 512, None),
                         ("fm", KV_OFF["ksel"], 384, 2), ("fm", KV_OFF["ksel"] + 384, 384, 3), ("tm", KV_OFF["vsel"], 512, 0),
                         ("fm", KV_OFF["kwin"], 384, 4), ("fm", KV_OFF["kwin"] + 384, 384, 5), ("tm", KV_OFF["vwin"], 512, 1)]

                def issue(i):
                    if i < len(tiles):
                        _, c0, ncol, _ = tiles[i]
                        kb.dma("pool", wt[i % 3][:, :, 0:ncol], Wkv[:, :, c0:c0 + ncol], None, b_wt[i % 3])
                issue(0); issue(1)
                vi = 0
                for i, (kind, c0, ncol, di) in enumerate(tiles):
                    issue(i + 2)
                    ws = i % 3
                    if kind == "fm":
                        fm_gemm(fb, wt[ws], b_wt[ws], [(0, 128), (128, 64), (192, 128), (320, 64)], fm_d[di], b_fm_d[di], 0)
                    elif kind == "vc":
                        fm_gemm(fb, wt[ws], b_wt[ws], [(0, 128), (128, 128), (256, 128), (384, 128)], vcT_d, b_vcT_d, 0)
                    else:
                        dst = (vs_d, vw_d)[di]; b_dst = (b_vs_d, b_vw_d)[di]
                        for t in range(NT):
                            pi = vi % 2; vi += 1
                            for c in range(16):
                                kb.op("pe", lambda e: e.matmul(pV[pi][:], xT[:, c, t * 128:(t + 1) * 128], wt[ws][:, c, :],
                                                               start=(c == 0), stop=(c == 15)), [b_xT, b_wt[ws]], [b_pV[pi]])
                            kb.op("act", lambda e: e.copy(vst[pi][:], pV[pi][:]), [b_pV[pi]], [b_vst[pi]])
                            kb.dma("sp", dst[t * 128:(t + 1) * 128, :], vst[pi][:], b_vst[pi], b_dst)
            kb.barrier()
            for i in range(6):
                kb.allgather(fm_d[i][:, :], fm_all[i][:, :], b_fm_d[i], b_fm_all[i], GROUPS)
            kb.allgather(vcT_d[:, :], vcT_all[:, :], b_vcT_d, b_vcT_all, GROUPS)
            kb.allgather(vs_d[:, :], vs_all[:, :], b_vs_d, b_vs_all, GROUPS)
            kb.allgather(vw_d[:, :], vw_all[:, :], b_vw_d, b_vw_all, GROUPS)

        def nsa_compress():
            with ExitStack() as ses:
                w1k = kb.sb(ses, "cp_w1k", [128, 32, 256], BF16); w1kb = kb.sb(ses, "cp_w1kb", [64, 32, 256], BF16)
                w1v = kb.sb(ses, "cp_w1v", [128, 32, 256], BF16)
                w2k = kb.sb(ses, "cp_w2k", [128, 2, 192], BF16); w2v = kb.sb(ses, "cp_w2v", [128, 2, 128], BF16)
                pekA = kb.sb(ses, "cp_pekA", [128, 32], BF16); pekB = kb.sb(ses, "cp_pekB", [64, 32], BF16); pev = kb.sb(ses, "cp_pev", [128, 32], BF16)
                b_cw = kb.buf("cp_w")
                ck1 = nsa_w_ck1.rearrange("(l d) h -> d l h", d=192)
                kb.dma("pool", w1k[:], ck1[0:128, :, :], None, b_cw)
                kb.dma("pool", w1kb[:], ck1[128:192, :, :], None, b_cw)
                kb.dma("pool", w1v[:], nsa_w_cv1.rearrange("(l d) h -> d l h", d=128), None, b_cw)
                kb.dma("pool", w2k[:], nsa_w_ck2.rearrange("(c p) n -> p c n", p=128), None, b_cw)
                kb.dma("pool", w2v[:], nsa_w_cv2.rearrange("(c p) n -> p c n", p=128), None, b_cw)
                pekT = nsa_pe_k.rearrange("l d -> d l")
                kb.dma("pool", pekA[:], pekT[0:128, :], None, b_cw, allow_slow_non_contiguous=True)
                kb.dma("pool", pekB[:], pekT[128:192, :], None, b_cw, allow_slow_non_contiguous=True)
                kb.dma("pool", pev[:], nsa_pe_v.rearrange("l d -> d l"), None, b_cw, allow_slow_non_contiguous=True)
                kA = kb.sb(ses, "cp_kA", [128, 4096], BF16); kB = kb.sb(ses, "cp_kB", [64, 4096], BF16)
                vA = kb.sb(ses, "cp_vA", [128, 4096], BF16)
                b_kA = kb.buf("cp_kA"); b_vA = kb.buf("cp_vA")
                hT = kb.sb(ses, "cp_hT", [128, 2, 256], BF16); b_hT = kb.buf("cp_hT")
                bias = kb.sb(ses, "cp_bias", [128, 4], F32); b_bias = kb.buf("cp_bias")
                pH = [kb.ps(ses, f"cp_pH{i}", [128, 512], F32) for i in range(2)]; b_pH = kb.bufs(2, "cp_pH")
                pB = kb.ps(ses, "cp_pB", [128, 512], F32); b_pB = kb.buf("cp_pB")
                pK = [kb.ps(ses, f"cp_pK{i}", [128, 512], F32) for i in range(2)]; b_pK = kb.bufs(2, "cp_pK")
                for hc in range(2):
                    n = 0
                    for l in range(32):
                        kb.op("pe", lambda e: e.matmul(pB[:, hc:hc + 1], w1k[:, l, hc * 128:(hc + 1) * 128], pekA[:, l:l + 1],
                                                       start=(n == 0), stop=False), [b_cw], [b_pB]); n += 1
                        kb.op("pe", lambda e: e.matmul(pB[:, hc:hc + 1], w1kb[:, l, hc * 128:(hc + 1) * 128], pekB[:, l:l + 1],
                                                       start=False, stop=(l == 31)), [b_cw], [b_pB])
                for hc in range(2):
                    for l in range(32):
                        kb.op("pe", lambda e: e.matmul(pB[:, 2 + hc:3 + hc], w1v[:, l, hc * 128:(hc + 1) * 128], pev[:, l:l + 1],
                                                       start=(l == 0), stop=(l == 31)), [b_cw], [b_pB])
                kb.op("act", lambda e: e.copy(bias[:], pB[:, 0:4]), [b_pB], [b_bias])
                kb.op("dve", lambda e: e.memset(vcx[:], 0.0), [], [b_kc])
                for nch in range(2):
                    kb.dma("sp", vcx[:, nch, :, 129:193], bc(ovl_in[:, nch, :], 1, 4), None, b_kc)
                kb.op("dve", lambda e: e.memset(vcx[:, :, :, 128:129], 1.0), [], [b_kc])
                ii = 0
                for g in range(4):
                    fi, gl = g // 2, g % 2
                    for qq in range(4):
                        kb.dma("sp", kA[:, qq * TOK:(qq + 1) * TOK], fm_all[fi][qq * 384 + gl * 192: qq * 384 + gl * 192 + 128, :], b_fm_all[fi], b_kA)
                        kb.dma("sp", kB[:, qq * TOK:(qq + 1) * TOK], fm_all[fi][qq * 384 + gl * 192 + 128: qq * 384 + gl * 192 + 192, :], b_fm_all[fi], b_kA)
                        kb.dma("sp", vA[:, qq * TOK:(qq + 1) * TOK], vcT_all[qq * 512 + g * 128: qq * 512 + (g + 1) * 128, :], b_vcT_all, b_vA)
                    for which in range(2):
                        for hc in range(2):
                            pp = pH[ii % 2]; b_pp = b_pH[ii % 2]; ii += 1
                            n = 0
                            tot = 64 if which == 0 else 32
                            for l in range(32):
                                srcs = [(w1k, kA, 128), (w1kb, kB, 64)] if which == 0 else [(w1v, vA, 128)]
                                for (wsrc, ksrc, kr) in srcs:
                                    rhs = bass.AP(ksrc[:].tensor, ksrc[0:kr, l:l + 1].offset, [list(ksrc[0:kr, :].ap[0]), [16, 255]])
                                    kb.op("pe", lambda e: e.matmul(pp[:, 0:255], wsrc[0:kr, l, hc * 128:(hc + 1) * 128], rhs,
                                                                   start=(n == 0), stop=(n == tot - 1)),
                                          [b_cw, b_kA if which == 0 else b_vA], [b_pp]); n += 1
                            kb.op("act", lambda e: e.activation(hT[:, hc, 0:255], pp[:, 0:255], AF.Silu, bias=bias[:, which * 2 + hc: which * 2 + hc + 1]),
                                  [b_pp, b_bias], [b_hT])
                        if which == 0:
                            for (r0, nr, dst) in ((0, 128, kcA), (128, 64, kcB)):
                                pp = pK[0]; b_pp = b_pK[0]
                                for hc in range(2):
                                    kb.op("pe", lambda e: e.matmul(pp[0:nr, 0:255], w2k[:, hc, r0:r0 + nr], hT[:, hc, 0:255],
                                                                   start=(hc == 0), stop=(hc == 1)), [b_cw, b_hT], [b_pp])
                                kb.op("act", lambda e: e.copy(dst[0:nr, g, 0:255], pp[0:nr, 0:255]), [b_pp], [b_kc])
                        else:
                            for nch in range(2):
                                nr = 128 if nch == 0 else 127
                                pp = pK[1]; b_pp = b_pK[1]
                                for hc in range(2):
                                    kb.op("pe", lambda e: e.matmul(pp[0:nr, 0:128], hT[:, hc, nch * 128: nch * 128 + nr], w2v[:, hc, :],
                                                                   start=(hc == 0), stop=(hc == 1)), [b_cw, b_hT], [b_pp])
                                kb.op("act", lambda e: e.copy(vcx[0:nr, nch, g, 0:128], pp[0:nr, 0:128]), [b_pp], [b_kc])
            kb.barrier()

        def nsa_proj(l):
            Win = nsa_w_in[l - 2].rearrange("(c p) n -> p c n", p=128)
            with ExitStack() as ses:
                wt = [kb.sb(ses, f"na_w{i}", [128, 16, 512], BF16) for i in range(3)]; b_wt = kb.bufs(3, "na_w")
                stage = [kb.sb(ses, f"na_st{i}", [128, TOK], BF16) for i in range(2)]; b_stage = kb.bufs(2, "na_st")
                pF = [kb.ps(ses, f"na_pF{i}", [128, 512], F32) for i in range(2)]; b_pF = kb.bufs(2, "na_pF")
                pV = [kb.ps(ses, f"na_pV{i}", [128, 512], F32) for i in range(4)]; b_pV = kb.bufs(4, "na_pV")
                gsig = kb.sb(ses, "na_gsig", [128, NT, 48], F32); b_gsig = kb.buf("na_gsig")
                zs = [kb.sb(ses, f"na_zs{i}", [128, 512], F32) for i in range(8)]; b_zs = kb.bufs(8, "na_zs")
                fb = (stage, b_stage, pF, b_pF, [0, 0])
                tiles = [("gate", 3072 + 6144, 48, None)] + [("q", j * 384, 384, j) for j in range(8)] + \
                        [("z", 3072 + br * 2048 + n * 512, 512, (br, n)) for br in range(3) for n in range(4)]

                def issue(i):
                    if i < len(tiles):
                        _, c0, ncol, _ = tiles[i]
                        kb.dma("pool", wt[i % 3][:, :, 0:ncol], Win[:, :, c0:c0 + ncol], None, b_wt[i % 3])
                issue(0); issue(1)
                vi = 0; zi = 0
                for i, (kind, c0, ncol, info) in enumerate(tiles):
                    issue(i + 2)
                    ws = i % 3
                    if kind == "gate":
                        for t in range(NT):
                            pi = vi % 4; vi += 1
                            for c in range(16):
                                kb.op("pe", lambda e: e.matmul(pV[pi][:, 0:48], xT[:, c, t * 128:(t + 1) * 128], wt[ws][:, c, 0:48],
                                                               start=(c == 0), stop=(c == 15)), [b_xT, b_wt[ws]], [b_pV[pi]])
                            kb.op("act", lambda e: e.activation(gsig[:, t, :], pV[pi][:, 0:48], AF.Sigmoid), [b_pV[pi]], [b_gsig])
                    elif kind == "q":
                        fm_gemm(fb, wt[ws], b_wt[ws], [(0, 128), (128, 64), (192, 128), (320, 64)], qn_d, b_qn_d, info * 384)
                    else:
                        br, n = info
                        for t in range(NT):
                            pi = vi % 4; vi += 1
                            for c in range(16):
                                kb.op("pe", lambda e: e.matmul(pV[pi][:], xT[:, c, t * 128:(t + 1) * 128], wt[ws][:, c, :],
                                                               start=(c == 0), stop=(c == 15)), [b_xT, b_wt[ws]], [b_pV[pi]])
                            z3 = zi % 8; zi += 1
                            kb.op("act", lambda e: e.activation(zs[z3][:], pV[pi][:], AF.Silu), [b_pV[pi]], [b_zs[z3]])
                            kb.op("dve", lambda e: e.tensor_tensor(zs[z3][:].rearrange("p (h e) -> p h e", h=4),
                                                                   zs[z3][:].rearrange("p (h e) -> p h e", h=4),
                                                                   bc(gsig[:, t, br * 16 + n * 4: br * 16 + n * 4 + 4], 2, 128), ALU.mult),
                                  [b_zs[z3], b_gsig], [b_zs[z3]])
                            kb.dma("sp", G_d[br][t * 128:(t + 1) * 128, n * 512:(n + 1) * 512], zs[z3][:], b_zs[z3], b_G_d)
            kb.barrier()

        def nsa_attn(l):
            xres_in_ap = xres_d[(l - 1) % 2]; b_xres_in = b_xres[(l - 1) % 2]
            xres_out_ap = xres_d[l % 2]; b_xres_out = b_xres[l % 2]
            ses_o = ExitStack()
            mixT = kb.sb(ses_o, "nb_mixT", [128, 16, TOK], BF16); b_mixT = kb.buf("nb_mixT")
            with ExitStack() as ses:
                cmaskT = kb.sb(ses, "nb_cmask", [128, 2, TOK], BF16)
                addtab = kb.sb(ses, "nb_add", [128, NT, 64], F32); valid = kb.sb(ses, "nb_valid", [128, NT, 64], F32)
                tabO = kb.sb(ses, "nb_tabO", [128, NT, 64], F32); tabL = kb.sb(ses, "nb_tabL", [128, NT, 64], F32)
                tri = kb.sb(ses, "nb_tri", [128, 128], BF16); triU = kb.sb(ses, "nb_triU", [128, 128], BF16)
                hsel = kb.sb(ses, "nb_hsel", [128, 4], F32)
                b_tab = kb.buf("nb_tab")
                for dst, src in ((cmaskT, cmask_in), (addtab, addtab_in), (valid, valid_in), (tabO, tabO_in), (tabL, tabL_in),
                                 (tri, tri_in), (triU, triU_in), (hsel, hsel_in)):
                    kb.dma("sp", dst[:], src, None, b_tab)
                KA = kb.sb(ses, "nb_KA", [128, 3072], BF16); KBf = kb.sb(ses, "nb_KB", [128, 3072], BF16)
                Vx = kb.sb(ses, "nb_Vx", [128, 24, 129], BF16)
                KAo = kb.sb(ses, "nb_KAo", [128, TOK], BF16); KBo = kb.sb(ses, "nb_KBo", [128, TOK], BF16)
                Vxo = kb.sb(ses, "nb_Vxo", [128, NT, 129], BF16)
                WA = kb.sb(ses, "nb_WA", [128, 512 + TOK], BF16); WB = kb.sb(ses, "nb_WB", [64, 512 + TOK], BF16)
                Wx = kb.sb(ses, "nb_Wx", [128, 4 + NT, 129], BF16)
                cA = kb.sb(ses, "nb_cA", [128, 4, 512], BF16); cB = kb.sb(ses, "nb_cB", [64, 4, 512], BF16)
                cV = kb.sb(ses, "nb_cV", [128, 4, 4, 129], BF16)
                b_K = kb.buf("nb_K"); b_V = kb.buf("nb_V"); b_W = kb.buf("nb_W"); b_cand = kb.buf("nb_cand")
                QA = [kb.sb(ses, f"nb_QA{i}", [128, TOK], BF16) for i in range(2)]
                QBo = [kb.sb(ses, f"nb_QBo{i}", [128, TOK], BF16) for i in range(2)]
                QBw = [kb.sb(ses, f"nb_QBw{i}", [128, TOK], BF16) for i in range(2)]
                b_Q = kb.bufs(2, "nb_Q")
                MbO = kb.sb(ses, "nb_MbO", [128, TOK], BF16); MbW = kb.sb(ses, "nb_MbW", [128, TOK], BF16); b_Mb = kb.buf("nb_Mb")
                Gh = [kb.sb(ses, f"nb_G{i}", [128, NT, 128], F32) for i in range(4)]; b_Gh = kb.bufs(4, "nb_G")
                mix = kb.sb(ses, "nb_mix", [128, 4, NT, 128], F32); b_mix = kb.buf("nb_mix")
                psel = kb.sb(ses, "nb_psel", [128, NT, 64], F32); b_psel = kb.buf("nb_psel")
                PT = [kb.sb(ses, f"nb_PT{i}", [128, TOK], BF16) for i in range(3)]; b_PT = kb.bufs(3, "nb_PT")
                rinv = [kb.sb(ses, f"nb_rinv{i}", [128, 16], F32) for i in range(2)]; b_rinv = kb.bufs(2, "nb_rinv")
                tmpo = [kb.sb(ses, f"nb_tmpo{i}", [128, 128], F32) for i in range(2)]; b_tmpo = kb.bufs(2, "nb_tmpo")
                sc = kb.sb(ses, "nb_sc", [128, 64], F32); wk = kb.sb(ses, "nb_wk", [128, 64], F32)
                m8 = kb.sb(ses, "nb_m8", [128, 24], F32); Mm = kb.sb(ses, "nb_Mm", [128, 64], F32)
                Mpad = [kb.sb(ses, f"nb_Mpad{i}", [128, 128], F32) for i in range(16)]
                ident32 = kb.sb(ses, "nb_id32", [128, 128], F32)
                kb.dma("sp", ident32[:], ident32_in[:, :], None, b_tab)
                b_tk = kb.buf("nb_tk"); b_Mpad = kb.bufs(16, "nb_Mpad")
                pS = [kb.ps(ses, f"nb_pS{i}", [128, 1024], F32) for i in range(2)]; b_pS = kb.bufs(2, "nb_pS")
                pOall = kb.ps(ses, "nb_pOall", [128, 2048], F32)
                pO = [pOall[:, i * 512:(i + 1) * 512] for i in range(4)]; b_pO = kb.bufs(4, "nb_pO")
                cnt = dict(s=0, pt=0, t=0, g=0, r=0, tm=0)
                for i in range(16):
                    kb.op("dve", lambda e: e.memset(Mpad[i][:], 0.0), [], [b_Mpad[i]])

                def oacc(qt):
                    return pO[qt // 2], b_pO[qt // 2], (qt % 2) * 256

                def segs(c0, ncol):
                    out = []
                    if c0 < 512:
                        out.append((c0, min(512, c0 + ncol)))
                    if c0 + ncol > 512:
                        out.append((max(512, c0), c0 + ncol))
                    return out

                deferred = []
                pend = []

                def flush(keep=0):
                    while len(pend) > keep:
                        pend.pop(0)()

                def attend(qa, qb, lA, lB, krB, vrhs, qlo, qhi, masks, reads, first, lastf):
                    si = cnt["s"] % 2; cnt["s"] += 1
                    pi = cnt["pt"] % 3; cnt["pt"] += 1
                    c0 = qlo * 128; ncol = (qhi - qlo + 1) * 128
                    for (x0, x1) in segs(c0, ncol):
                        kb.op("pe", lambda e: e.matmul(pS[si][:, x0:x1], lA, qa[:, x0:x1], start=True, stop=False), reads, [b_pS[si]])
                        kb.op("pe", lambda e: e.matmul(pS[si][:, x0:x1], lB, qb[0:krB, x0:x1], start=False, stop=True), reads, [b_pS[si]])
                    kb.op("act", lambda e: e.activation(PT[pi][:, c0:c0 + ncol], pS[si][:, c0:c0 + ncol], AF.Exp, scale=QSCALE), [b_pS[si]], [b_PT[pi]])
                    for qt, m in masks.items():
                        kb.op("dve", lambda e: e.tensor_tensor(PT[pi][:, qt * 128:(qt + 1) * 128], PT[pi][:, qt * 128:(qt + 1) * 128], m, ALU.mult),
                              [b_PT[pi], b_tab], [b_PT[pi]])

                    def pv():
                        for qt in range(qlo, qhi + 1):
                            acc, b_acc, co = oacc(qt)
                            kb.op("pe", lambda e: e.matmul(acc[:, co:co + 129], PT[pi][:, qt * 128:(qt + 1) * 128], vrhs,
                                                           start=(first[qt] and qt % 2 == 0), stop=lastf(qt)), [b_PT[pi]] + reads, [b_acc])
                            first[qt] = False
                    flush(1)
                    pend.append(pv)

                def finish(hl, Gt, b_Gt, add):
                    flush(0)
                    ri = cnt["r"] % 2; cnt["r"] += 1
                    rs_ap = pOall[:].rearrange("p (b o c) -> p b o c", b=4, o=2)[:, :, :, 128:129]
                    kb.op("dve", lambda e: e.tensor_scalar(rinv[ri][:, 0:8].rearrange("p (b o c) -> p b o c", b=4, o=2), rs_ap, 1e-30, None, ALU.add),
                          b_pO, [b_rinv[ri]])
                    kb.op("dve", lambda e: e.reciprocal(rinv[ri][:, 8:16], rinv[ri][:, 0:8]), [b_rinv[ri]], [b_rinv[ri]])
                    for qt in range(NT):
                        acc, b_acc, co = oacc(qt)
                        t = qt
                        if not add:
                            kb.op("dve", lambda e: e.scalar_tensor_tensor(mix[:, hl, t, :], acc[:, co:co + 128], rinv[ri][:, 8 + qt:9 + qt], Gt[:, t, :],
                                                                          ALU.mult, ALU.mult), [b_acc, b_rinv[ri], b_Gt], [b_mix])
                        else:
                            ti = cnt["tm"] % 2; cnt["tm"] += 1
                            kb.op("dve", lambda e: e.scalar_tensor_tensor(tmpo[ti][:], acc[:, co:co + 128], rinv[ri][:, 8 + qt:9 + qt], Gt[:, t, :],
                                                                          ALU.mult, ALU.mult), [b_acc, b_rinv[ri], b_Gt], [b_tmpo[ti]])
                            kb.op("dve", lambda e: e.tensor_tensor(mix[:, hl, t, :], mix[:, hl, t, :], tmpo[ti][:], ALU.add), [b_mix, b_tmpo[ti]], [b_mix])
                    return ri

                def load_G(br, h):
                    gi = cnt["g"] % 4; cnt["g"] += 1
                    kb.dma("sp", Gh[gi][:], G_d[br][:, h * 128:(h + 1) * 128].rearrange("(t p) e -> p t e", p=128), b_G_d, b_Gh[gi])
                    return Gh[gi], b_Gh[gi]

                def load_q(h, s):
                    kb.dma("sp", QA[s][:], qn_d[h * 192:h * 192 + 128, :], b_qn_d, b_Q[s])
                    kb.dma("sp", QBo[s][0:64, :], qn_d[h * 192 + 128:h * 192 + 192, :], b_qn_d, b_Q[s])
                    kb.dma("sp", QBw[s][0:64, :], qn_d[h * 192 + 128:h * 192 + 192, :], b_qn_d, b_Q[s])

                hq = 0
                for g in range(4):
                    fi, gl = g // 2, g % 2
                    rA = gl * 192
                    for qq in range(3):
                        kb.dma("sp", KA[:, qq * TOK:(qq + 1) * TOK], fm_all[2 + fi][qq * 384 + rA: qq * 384 + rA + 128, :], b_fm_all[2 + fi], b_K)
                        kb.dma("sp", KBf[0:64, qq * TOK:(qq + 1) * TOK], fm_all[2 + fi][qq * 384 + rA + 128: qq * 384 + rA + 192, :], b_fm_all[2 + fi], b_K)
                    kb.dma("sp", KBf[64:128, :], eall_in[:, 0:3072], None, b_K)
                    kb.dma("sp", KAo[:], fm_d[2 + fi][rA:rA + 128, :], b_fm_d[2 + fi], b_K)
                    kb.dma("sp", KBo[0:64, :], fm_d[2 + fi][rA + 128:rA + 192, :], b_fm_d[2 + fi], b_K)
                    kb.dma("sp", KBo[64:128, :], eown_in[:, :], None, b_K)
                    for qq in range(3):
                        kb.dma("sp", Vx[:, qq * 8:(qq + 1) * 8, 0:128], vs_all[qq * TOK:(qq + 1) * TOK, g * 128:(g + 1) * 128].rearrange("(t p) e -> p t e", p=128), b_vs_all, b_V)
                    kb.dma("sp", Vxo[:, :, 0:128], vs_d[:, g * 128:(g + 1) * 128].rearrange("(t p) e -> p t e", p=128), b_vs_d, b_V)
                    kb.op("dve", lambda e: e.memset(Vx[:, :, 128:129], 1.0), [], [b_V])
                    kb.op("dve", lambda e: e.memset(Vxo[:, :, 128:129], 1.0), [], [b_V])
                    kb.dma("sp", WA[:, 512:], fm_d[4 + fi][rA:rA + 128, :], b_fm_d[4 + fi], b_W)
                    kb.dma("sp", WB[:, 512:], fm_d[4 + fi][rA + 128:rA + 192, :], b_fm_d[4 + fi], b_W)
                    kb.dma("sp", Wx[:, 4:, 0:128], vw_d[:, g * 128:(g + 1) * 128].rearrange("(t p) e -> p t e", p=128), b_vw_d, b_W)
                    for qq in range(4):
                        kb.dma("sp", cA[:, qq, :], fm_all[4 + fi][qq * 384 + rA: qq * 384 + rA + 128, 512:1024], b_fm_all[4 + fi], b_cand)
                        kb.dma("sp", cB[:, qq, :], fm_all[4 + fi][qq * 384 + rA + 128: qq * 384 + rA + 192, 512:1024], b_fm_all[4 + fi], b_cand)
                        kb.dma("sp", cV[:, qq, :, 0:128], vw_all[qq * TOK + 512:(qq + 1) * TOK, g * 128:(g + 1) * 128].rearrange("(t p) e -> p t e", p=128),
                               b_vw_all, b_cand)
                    kb.op("dve", lambda e: e.memset(cV[:, :, :, 128:129], 1.0), [], [b_cand])
                    kb.op("dve", lambda e: e.memset(Wx[:, 4:, 128:129], 1.0), [], [b_W])
                    for (dst, src, np_) in ((WA[:, 0:512], lambda qq: cA[:, qq, :], 128), (WB[:, 0:512], lambda qq: cB[:, qq, :], 64),
                                            (Wx[:, 0:4, :], lambda qq: cV[:, qq, :, :], 128)):
                        kb.op("dve", lambda e: e.tensor_scalar(dst, src(0), hsel[0:np_, 0:1], None, ALU.mult), [b_cand, b_tab], [b_W])
                        for qq in range(1, 4):
                            kb.op("dve", lambda e: e.scalar_tensor_tensor(dst, src(qq), hsel[0:np_, qq:qq + 1], dst, ALU.mult, ALU.add), [b_cand, b_tab, b_W], [b_W])
                    while deferred:
                        deferred.pop(0)()
                    kb.op("dve", lambda e: e.memset(psel[:], 0.0), [], [b_psel])
                    for hl in range(4):
                        h = g * 4 + hl
                        s = hq % 2; hq += 1
                        load_q(h, s)
                        Gt, b_Gt = load_G(0, h)
                        for nch in range(2):
                            nr = 128 if nch == 0 else 127
                            si = cnt["s"] % 2; cnt["s"] += 1
                            pi = cnt["pt"] % 3; cnt["pt"] += 1
                            for (x0, x1) in ((0, 512), (512, 1024)):
                                kb.op("pe", lambda e: e.matmul(pS[si][0:nr, x0:x1], kcA[:, g, nch * 128:nch * 128 + nr], QA[s][:, x0:x1], start=True, stop=False),
                                      [b_kc, b_Q[s]], [b_pS[si]])
                                kb.op("pe", lambda e: e.matmul(pS[si][0:nr, x0:x1], kcB[0:64, g, nch * 128:nch * 128 + nr], QBo[s][0:64, x0:x1], start=False, stop=True),
                                      [b_kc, b_Q[s]], [b_pS[si]])
                            kb.op("act", lambda e: e.activation(PT[pi][0:nr, :], pS[si][0:nr, :], AF.Exp, scale=QSCALE), [b_pS[si]], [b_PT[pi]])
                            kb.op("dve", lambda e: e.tensor_tensor(PT[pi][0:nr, :], PT[pi][0:nr, :], cmaskT[0:nr, nch, :], ALU.mult), [b_PT[pi], b_tab], [b_PT[pi]])

                            def pvc(nr=nr, nch=nch, pi=pi):
                                for qt in range(NT):
                                    acc, b_acc, co = oacc(qt)
                                    kb.op("pe", lambda e: e.matmul(acc[:, co:co + 193], PT[pi][0:nr, qt * 128:(qt + 1) * 128], vcx[0:nr, nch, g, :],
                                                                   start=(nch == 0 and qt % 2 == 0), stop=(nch == 1)), [b_PT[pi], b_kc], [b_acc])
                            flush(1)
                            pend.append(pvc)
                        ri = finish(hl, Gt, b_Gt, add=False)
                        for qt in range(NT):
                            acc, b_acc, co = oacc(qt)
                            kb.op("dve", lambda e: e.scalar_tensor_tensor(psel[:, qt, :], acc[:, co + 129:co + 193], rinv[ri][:, 8 + qt:9 + qt], psel[:, qt, :],
                                                                          ALU.mult, ALU.add), [b_acc, b_rinv[ri], b_psel], [b_psel])
                    def topk_dve(t):
                        kb.op("dve", lambda e: e.tensor_tensor(sc[:], psel[:, t, :], valid[:, t, :], ALU.mult), [b_psel, b_tab], [b_tk])
                        kb.op("dve", lambda e: e.tensor_tensor(sc[:], sc[:], addtab[:, t, :], ALU.add), [b_tk, b_tab], [b_tk])
                        kb.op("dve", lambda e: e.max(m8[:, 0:8], sc[:]), [b_tk], [b_tk])
                        kb.op("dve", lambda e: e.match_replace(wk[:], m8[:, 0:8], sc[:], -1e30), [b_tk], [b_tk])
                        kb.op("dve", lambda e: e.max(m8[:, 8:16], wk[:]), [b_tk], [b_tk])
                        kb.op("dve", lambda e: e.tensor_reduce(m8[:, 16:17], m8[:, 8:16], mybir.AxisListType.X, ALU.min), [b_tk], [b_tk])
                        kb.op("dve", lambda e: e.tensor_scalar(Mm[:], sc[:], m8[:, 16:17], None, ALU.is_ge), [b_tk], [b_tk])
                        for which, tab in ((0, tabO), (1, tabL)):
                            mi = t * 2 + which
                            kb.op("dve", lambda e: e.tensor_tensor(wk[:], Mm[:], tab[:, t, :], ALU.mult), [b_tk, b_tab], [b_tk])
                            kb.op("dve", lambda e: e.tensor_scalar(Mpad[mi][:, 64:128], wk[:], BIG, -BIG, ALU.mult, ALU.add), [b_tk], [b_Mpad[mi]])

                    def topk_pe(t):
                        for which, dstM in ((0, MbO), (1, MbW)):
                            mi = t * 2 + which
                            si = cnt["s"] % 2; cnt["s"] += 1
                            kb.op("pe", lambda e: e.transpose(pS[si][:, 0:128], Mpad[mi][:], ident32[:]), [b_Mpad[mi], b_tab], [b_pS[si]])
                            kb.op("act", lambda e: e.copy(dstM[64:128, t * 128:(t + 1) * 128], pS[si][64:128, 0:128]), [b_pS[si]], [b_Mb])
                    for t in range(NT):
                        topk_dve(t)
                    for hl in range(4):
                        h = g * 4 + hl
                        s = hq % 2; hq += 1
                        load_q(h, s)
                        Gs, b_Gs = load_G(1, h)
                        Gw, b_Gw = load_G(2, h)
                        rQ = [b_Q[s]]
                        first = {qt: True for qt in range(NT)}
                        for wkt in range(12):
                            ilo = max(wkt - 4, 0); ihi = min(wkt, NT - 1)
                            masks = {}
                            if wkt - 4 >= 0:
                                masks[wkt - 4] = tri[:]
                            if wkt <= NT - 1:
                                masks[wkt] = triU[:]
                            attend(QA[s], QBw[s], WA[:, wkt * 128:(wkt + 1) * 128], WB[0:64, wkt * 128:(wkt + 1) * 128], 64, Wx[:, wkt, :],
                                   ilo, ihi, masks, rQ + [b_W], first, lambda qt, wkt=wkt: wkt == qt + 4)
                        finish(hl, Gw, b_Gw, add=True)
                        if hl == 0:
                            for t in range(NT):
                                topk_pe(t)
                        kb.op("act", lambda e: e.copy(QBo[s][64:128, :], MbO[64:128, :]), [b_Mb], [b_Q[s]])
                        kb.op("act", lambda e: e.copy(QBw[s][64:128, :], MbW[64:128, :]), [b_Mb], [b_Q[s]])
                        first = {qt: True for qt in range(NT)}
                        for kt in range(24):
                            attend(QA[s], QBo[s], KA[:, kt * 128:(kt + 1) * 128], KBf[:, kt * 128:(kt + 1) * 128], 128, Vx[:, kt, :], 0, NT - 1, {},
                                   rQ + [b_K, b_V], first, lambda qt: False)
                            if kt == 6:
                                while deferred:
                                    deferred.pop(0)()
                        for ktl in range(NT):
                            attend(QA[s], QBw[s], KAo[:, ktl * 128:(ktl + 1) * 128], KBo[:, ktl * 128:(ktl + 1) * 128], 128, Vxo[:, ktl, :], ktl, NT - 1,
                                   {ktl: tri[:]}, rQ + [b_K, b_V], first, lambda qt, ktl=ktl: qt == ktl)
                        finish(hl, Gs, b_Gs, add=True)
                        def emit_mix(h=h, hl=hl):
                            si = cnt["s"] % 2; cnt["s"] += 1
                            for t in range(NT):
                                kb.op("pe", lambda e: e.transpose(pS[si][:, t * 128:(t + 1) * 128], mix[:, hl, t, :], ident32[:]), [b_mix, b_tab], [b_pS[si]])
                            kb.op("act", lambda e: e.copy(mixT[:, h, :], pS[si][:, :]), [b_pS[si]], [b_mixT])
                        deferred.append(emit_mix)
                while deferred:
                    deferred.pop(0)()
            kb.barrier()
            with ExitStack() as ses:
                layer_epilogue(l, ses, mixT, b_mixT, 16, nsa_w_out[l - 2], xres_in_ap, b_xres_in, xres_out_ap, b_xres_out)
            kb.barrier()
            ses_o.close()

        for l in range(min(n_layers, 2)):
            retention_layer(l)
        if NSA:
            nsa_kv()
            nsa_proj(2)
            cc_fence()
            nsa_compress()
            nsa_attn(2)
            for l in range(3, n_layers):
                nsa_proj(l)
                nsa_attn(l)

        kb.wait_all("sp", [b_y])
        if debug:
            kb.wait_all("sp", [P.b_dbg, b_qT_d, b_kT_d, b_v_d, b_z_d, b_vres])
        P.stats = dict(cnt=dict(kb.cnt), nwait=kb.nwait, nsem=kb.nsem)
    return nc, P


def _perm_ret_w_in(w):
    w = np.asarray(w)
    out = w.copy()
    idx = np.concatenate([np.arange(0, 256, 2), np.arange(1, 256, 2)])
    for blk in range(2):
        for h in range(RH):
            base = blk * 2048 + h * 256
            out[:, base:base + 256] = w[:, base + idx]
    return out


def _nsa_tables(q):
    bf = ml_dtypes.bfloat16
    tok = np.arange(TOK)
    sg = q * TOK + tok
    n = np.arange(256)
    cm = ((16 * n[:, None] + 31) <= sg[None, :]) & (n[:, None] < 255)
    cmask = np.ascontiguousarray(cm.reshape(2, 128, TOK).transpose(1, 0, 2)).astype(np.float32).astype(bf)
    j = np.arange(64)
    cur = sg // 64
    forced = (j[None, :] == 0) | (j[None, :] == cur[:, None]) | (j[None, :] == cur[:, None] - 1)
    le = j[None, :] <= cur[:, None]
    add = np.where(forced, 1e9, np.where(le, 0.0, -1.0)).astype(np.float32)
    valid = (le & ~forced).astype(np.float32)
    tabO = (le & (j[None, :] < 16 * q)).astype(np.float32)
    tabL = le.astype(np.float32)
    tm = lambda a: np.ascontiguousarray(a.reshape(NT, 128, 64).transpose(1, 0, 2))
    k = np.arange(128)
    tri = (k[:, None] <= k[None, :]).astype(np.float32).astype(bf)
    triU = (k[:, None] > k[None, :]).astype(np.float32).astype(bf)
    hsel = np.zeros((128, 4), np.float32)
    if q > 0:
        hsel[:, q - 1] = 1.0
    kk = np.arange(4096)
    eall = (kk[None, :] // 64 == j[:, None]).astype(np.float32).astype(bf)
    kl = np.arange(TOK)
    eown = ((16 * q + kl[None, :] // 64) == j[:, None]).astype(np.float32).astype(bf)
    diff = n[:, None] - 4 * j[None, :]
    ov = np.where((diff >= 0) & (diff <= 4), np.minimum(np.minimum(diff, 4 - diff), 1) + 1, 0).astype(np.float32)
    ov[255] = 0
    ovl = np.ascontiguousarray(ov.reshape(2, 128, 64).transpose(1, 0, 2)).astype(bf)
    return dict(cmask=cmask, addtab=tm(add), valid=tm(valid), tabO=tm(tabO), tabL=tm(tabL), tri=tri, triU=triU,
                hsel=hsel, eall=eall, eown=eown, ovl=ovl)


def make_in_maps(inputs, n_layers=4):
    lg, decayT, dq, dkc, dkf, g128 = _ret_tables()
    ident = np.eye(128, dtype=np.float32).astype(ml_dtypes.bfloat16)
    invf = (1.0 / (np.float32(10000.0) ** np.linspace(0.0, 1.0, 128, dtype=np.float32))).astype(np.float32).reshape(128, 1)
    x = np.asarray(inputs["x"], dtype=np.float32)
    pos = np.asarray(inputs["positions"]).astype(np.int32)
    shared = {"ident": ident, "ident32": np.eye(128, dtype=np.float32), "invf": invf, "decayT": decayT, "dq": dq, "dkc": dkc, "dkf": dkf}
    for l in range(min(2, n_layers)):
        shared[f"rwin{l}"] = _perm_ret_w_in(inputs[f"ret_w_in_{l}"])
        shared[f"rwout{l}"] = np.ascontiguousarray(np.asarray(inputs[f"ret_w_out_{l}"], dtype=np.float32))
    for l in range(n_layers):
        shared[f"lng{l}"] = np.asarray(inputs[f"ln_g_{l}"], dtype=np.float32).reshape(1, D)
        shared[f"lnb{l}"] = np.asarray(inputs[f"ln_b_{l}"], dtype=np.float32).reshape(1, D)
    if n_layers > 2:
        for nm in ("nsa_w_kv", "nsa_pe_k", "nsa_pe_v", "nsa_w_ck1", "nsa_w_ck2", "nsa_w_cv1", "nsa_w_cv2"):
            shared[nm] = np.ascontiguousarray(np.asarray(inputs[nm], dtype=np.float32))
        for l in range(2, n_layers):
            shared[f"nwin{l}"] = np.ascontiguousarray(np.asarray(inputs[f"nsa_w_in_{l}"], dtype=np.float32))
            shared[f"nwout{l}"] = np.ascontiguousarray(np.asarray(inputs[f"nsa_w_out_{l}"], dtype=np.float32))
    maps = []
    for core in range(8):
        b, q = core // 4, core % 4
        m = dict(shared)
        if n_layers > 2:
            m.update(_nsa_tables(q))
        m["x"] = np.ascontiguousarray(x[b, q * TOK:(q + 1) * TOK, :])
        m["pos"] = np.ascontiguousarray(pos[b, q * TOK:(q + 1) * TOK]).reshape(1, TOK)
        m["coef"] = _coef(lg, q)
        maps.append(m)
    return maps


def run(inputs, n_layers=4, trace=False, debug=False):
    nc, P = build_program(n_layers, debug)
    maps = make_in_maps(inputs, n_layers)
    res = run_bass_kernel_spmd(nc, maps, core_ids=list(range(8)))
    out = np.zeros((2, 4096, D), np.float32)
    for core in range(8):
        b, q = core // 4, core % 4
        out[b, q * TOK:(q + 1) * TOK, :] = res.results[core]["y"]
    if debug:
        P.dbg = [{k: np.asarray(res.results[c][k]) for k in P.dbg_names} for c in range(8)]
    return out, P


def kernel(**inputs):
    out, _ = run(inputs, 4)
    return out
```

```python
import numpy as np
import ml_dtypes
from contextlib import ExitStack
import concourse.bass as bass
import concourse.mybir as mybir
from concourse.bass_utils import run_bass_kernel_spmd

F32 = mybir.dt.float32
BF16 = mybir.dt.bfloat16
I32 = mybir.dt.int32
ALU = mybir.AluOpType
AF = mybir.ActivationFunctionType

D = 2048
TOK = 1024
NT = 8
ALPHA = 8.0 ** 0.25
LN_EPS = 1e-5
RH = 8
EPOCH = 12000
DLIMIT = 12000


class Buf:
    __slots__ = ("name", "w", "rd", "slot")

    def __init__(self, name):
        self.name = name
        self.w = None
        self.rd = {}
        self.slot = None


class KB:
    def __init__(self, nc, es):
        self.nc = nc
        self.es = es
        self.eng = {"pe": nc.tensor, "act": nc.scalar, "dve": nc.vector,
                    "pool": nc.gpsimd, "sp": nc.sync}
        self.sems = {e: [] for e in self.eng}
        self.cnt = {e: 0 for e in self.eng}
        self.seen = {e: {} for e in self.eng}
        self.slots = []
        self.free_slots_k = {}
        self.local_bufs = []
        self.persistent_mode = True
        self.nbuf = 0
        self.nwait = 0
        self.nsem = 0

    def _newsem(self, name):
        self.nsem += 1
        return self.es.enter_context(self.nc.semaphore(f"{name}_{self.nsem}"))

    def sb(self, es, name, shape, dt):
        self.nalloc = getattr(self, "nalloc", 0) + 1
        return es.enter_context(self.nc.sbuf_tensor(f"{name}_{self.nalloc}", list(shape), dt))

    def ps(self, es, name, shape, dt):
        self.nalloc = getattr(self, "nalloc", 0) + 1
        return es.enter_context(self.nc.psum_tensor(f"{name}_{self.nalloc}", list(shape), dt))

    def buf(self, name=None):
        self.nbuf += 1
        b = Buf(name or f"b{self.nbuf}")
        if not getattr(self, "persistent_mode", False):
            self.local_bufs.append(b)
        return b

    def bufs(self, n, name="b"):
        return [self.buf(f"{name}{i}") for i in range(n)]

    def _esem(self, e, n):
        k = (n - 1) // EPOCH
        while len(self.sems[e]) <= k:
            self.sems[e].append(self._newsem("s_" + e))
        return self.sems[e][k], n - k * EPOCH

    def _slot(self, b, q="sp"):
        if b.slot is None:
            kind = "sw" if q == "pool" else ("cc" if q == "cc" else "hw")
            fl = self.free_slots_k.setdefault(kind, [])
            if fl:
                s = fl.pop(0)
            elif len(self.slots) < 80:
                s = {"sem": self._newsem("d"), "val": 0, "id": len(self.slots), "kind": kind}
                self.slots.append(s)
            else:
                cand = [x for x in self.slots if x["kind"] == kind]
                self.nshare = getattr(self, "nshare", 0) + 1
                s = cand[self.nshare % len(cand)]
            b.slot = s
        return b.slot

    def _wait_tok(self, e, tok):
        if tok[0] == "e":
            key, val = tok[1], tok[2]
            if self.seen[e].get(key, 0) >= val:
                return
            sem, v = self._esem(tok[1], val)
            self.eng[e].wait_ge(sem, v)
        else:
            s = tok[1]
            key, val = ("d", s["id"], id(s["sem"])), s["val"]
            if self.seen[e].get(key, 0) >= val:
                return
            self.eng[e].wait_ge(s["sem"], val)
        self.seen[e][key] = val
        self.nwait += 1

    def _deps(self, e, reads, writes, is_dma=False):
        for b in reads:
            if b.w is not None:
                self._wait_tok(e, b.w)
        strict = e in ("act", "dve")
        for b in writes:
            if b.w is not None and (strict or not (b.w[0] == "e" and b.w[1] == e)) and not (is_dma and b.w[0] == "d"):
                self._wait_tok(e, b.w)
            for r in b.rd.values():
                if r[0] == "e" and r[1] == e and not strict:
                    continue
                self._wait_tok(e, r)

    def op(self, e, fn, reads=(), writes=()):
        self._deps(e, reads, writes)
        inst = fn(self.eng[e])
        self.cnt[e] += 1
        sem, _ = self._esem(e, self.cnt[e])
        inst.then_inc(sem, 1)
        t = ("e", e, self.cnt[e])
        for b in reads:
            b.rd[e] = t
        for b in writes:
            b.w = t
            b.rd = {}
        return inst

    def _slot_inc(self, q, s, amount):
        if s["val"] + amount > DLIMIT:
            self._wait_tok(q, ("d", s))
            s["sem"] = self._newsem("d")
            s["val"] = 0
        s["val"] += amount

    def dma(self, q, out_ap, in_ap, src, dst, **kw):
        reads = [src] if src is not None else []
        self._deps(q, reads, [dst], is_dma=True)
        s = self._slot(dst, q)
        self._slot_inc(q, s, 16)
        inst = self.eng[q].dma_start(out=out_ap, in_=in_ap, **kw)
        inst.then_inc(s["sem"], 16)
        t = ("d", s)
        if src is not None:
            src.rd[("d", s["id"])] = t
        dst.w = t
        dst.rd = {}
        return inst

    def allgather(self, in_ap, out_ap, src, dst, groups):
        self._deps("pool", [src], [dst])
        s = self._slot(dst, "cc")
        self._slot_inc("pool", s, 1)
        inst = self.nc.gpsimd.collective_compute("AllGather", ALU.bypass, replica_groups=groups,
                                                 ins=[in_ap.opt()], outs=[out_ap.opt()])
        inst.then_inc(s["sem"], 1)
        t = ("d", s)
        src.rd[("d", s["id"])] = t
        dst.w = t
        dst.rd = {}

    def barrier(self):
        for e in self.eng:
            for x in ("pe", "act", "dve"):
                if self.cnt[x] > 0:
                    self._wait_tok(e, ("e", x, self.cnt[x]))
            for sl in self.slots:
                if sl["val"] > 0:
                    self._wait_tok(e, ("d", sl))
        for b in self.local_bufs:
            if b.slot is not None:
                fl = self.free_slots_k.setdefault(b.slot["kind"], [])
                if b.slot not in fl:
                    fl.append(b.slot)
            b.slot = None
        self.local_bufs = []

    def wait_all(self, e, bufs):
        for b in bufs:
            if b.w is not None:
                self._wait_tok(e, b.w)


def bc(ap, dim, n):
    l = [list(x) for x in ap.ap]
    l.insert(dim, [0, n])
    return bass.AP(ap.tensor, ap.offset, l)


def _ret_tables():
    h = np.arange(RH, dtype=np.float64)
    lg = np.log1p(-np.exp2(-5.0 - h))
    c = np.arange(128, dtype=np.float64)
    rel = c[None, :] - c[:, None]
    decayT = np.where(rel[None] >= 0, np.exp(lg[:, None, None] * np.maximum(rel[None], 0)), 0.0) / 16.0
    decayT = np.ascontiguousarray(decayT.transpose(1, 0, 2)).astype(np.float32)
    dq = np.exp(lg[:, None] * (c[None, :] + 1.0))
    dq = np.broadcast_to(dq[None], (128, RH, 128)).astype(np.float32).copy()
    dkc = (np.exp(lg[None, :] * (127.0 - c[:, None])) / 16.0).astype(np.float32)
    t = np.arange(NT, dtype=np.float64)
    pos = t[None, None, :] * 128 + c[:, None, None]
    dkf = (np.exp(lg[None, :, None] * (1023.0 - pos)) / 16.0).astype(np.float32)
    g128 = np.exp(lg * 128.0)
    return lg, decayT, dq, dkc, dkf, g128


def _coef(lg, quarter):
    co = np.zeros((RH, 4), np.float64)
    for j in range(4):
        if j < quarter:
            co[:, j] = np.exp(lg * 1024.0 * (quarter - 1 - j))
    return np.broadcast_to(co[None], (128, RH, 4)).astype(np.float32).copy()


class Prog:
    pass


def build_program(n_layers=4, debug=False):
    nc = bass.Bass("TRN2", target_bir_lowering=False)
    P = Prog()
    dt_in = lambda name, shape, dt=F32: nc.dram_tensor(name, list(shape), dt, kind="ExternalInput").ap()
    dt_int = lambda name, shape, dt=F32: nc.dram_tensor(name, list(shape), dt, kind="Internal").ap()
    dt_dbg = lambda name, shape, dt=F32: nc.dram_tensor(name, list(shape), dt, kind=("ExternalOutput" if debug else "Internal")).ap()
    P.dbg_names = []

    x_in = dt_in("x", [TOK, D])
    pos_in = dt_in("pos", [1, TOK], I32)
    ident_in = dt_in("ident", [128, 128], BF16)
    ident32_in = dt_in("ident32", [128, 128], F32)
    invf_in = dt_in("invf", [128, 1])
    decayT_in = dt_in("decayT", [128, RH, 128])
    dq_in = dt_in("dq", [128, RH, 128])
    dkc_in = dt_in("dkc", [128, RH])
    dkf_in = dt_in("dkf", [128, RH, NT])
    coef_in = dt_in("coef", [128, RH, 4])
    w_in = [dt_in(f"rwin{l}", [D, 12288]) for l in range(min(2, n_layers))]
    w_out = [dt_in(f"rwout{l}", [4096, D]) for l in range(min(2, n_layers))]
    ln_g = [dt_in(f"lng{l}", [1, D]) for l in range(n_layers)]
    ln_b = [dt_in(f"lnb{l}", [1, D]) for l in range(n_layers)]
    y_out = nc.dram_tensor("y", [TOK, D], F32, kind="ExternalOutput").ap()

    P.dbg_names.append("qT_d")
    qT_d = dt_dbg("qT_d", [RH * 2 * 128, TOK], BF16)
    P.dbg_names.append("kT_d")
    kT_d = dt_dbg("kT_d", [RH * 2 * 128, TOK], BF16)
    P.dbg_names.append("v_d")
    v_d = dt_dbg("v_d", [TOK, 4096], BF16)
    P.dbg_names.append("z_d")
    z_d = dt_dbg("z_d", [TOK, 4096], F32)
    sloc_d = [dt_int(f"sloc_d{c}", [1024, 512], BF16) for c in range(2)]
    sloc_all = [dt_int(f"sloc_all{c}", [4 * 1024, 512], BF16) for c in range(2)]
    xres_d = [dt_int(f"xres{i}", [TOK, D], F32) for i in range(2)]
    P.dbg_names.append("vres_d")
    vres_d = dt_dbg("vres_d", [TOK, D], F32)

    NSA = n_layers > 2
    if NSA:
        nsa_w_kv = dt_in("nsa_w_kv", [D, 3840])
        nsa_pe_k = dt_in("nsa_pe_k", [32, 192]); nsa_pe_v = dt_in("nsa_pe_v", [32, 128])
        nsa_w_ck1 = dt_in("nsa_w_ck1", [6144, 256]); nsa_w_ck2 = dt_in("nsa_w_ck2", [256, 192])
        nsa_w_cv1 = dt_in("nsa_w_cv1", [4096, 256]); nsa_w_cv2 = dt_in("nsa_w_cv2", [256, 128])
        nsa_w_in = [dt_in(f"nwin{l}", [D, 9264]) for l in range(2, n_layers)]
        nsa_w_out = [dt_in(f"nwout{l}", [D, D]) for l in range(2, n_layers)]
        cmask_in = dt_in("cmask", [128, 2, TOK], BF16)
        addtab_in = dt_in("addtab", [128, NT, 64]); valid_in = dt_in("valid", [128, NT, 64])
        tabO_in = dt_in("tabO", [128, NT, 64]); tabL_in = dt_in("tabL", [128, NT, 64])
        tri_in = dt_in("tri", [128, 128], BF16); triU_in = dt_in("triU", [128, 128], BF16)
        hsel_in = dt_in("hsel", [128, 4])
        eall_in = dt_in("eall", [64, 4096], BF16); eown_in = dt_in("eown", [64, TOK], BF16)
        ovl_in = dt_in("ovl", [128, 2, 64], BF16)
        fm_d = [dt_int(f"fm_d{i}", [384, TOK], BF16) for i in range(6)]
        fm_all = [dt_int(f"fm_all{i}", [4 * 384, TOK], BF16) for i in range(6)]
        vcT_d = dt_int("vcT_d", [512, TOK], BF16); vcT_all = dt_int("vcT_all", [4 * 512, TOK], BF16)
        vs_d = dt_int("vs_d", [TOK, 512], BF16); vs_all = dt_int("vs_all", [4 * TOK, 512], BF16)
        vw_d = dt_int("vw_d", [TOK, 512], BF16); vw_all = dt_int("vw_all", [4 * TOK, 512], BF16)
        qn_d = dt_int("qn_d", [16 * 192, TOK], BF16)
        G_d = [dt_int(f"G_d{i}", [TOK, D], F32) for i in range(3)]
    fence_d = dt_int("fence_d", [16, 64], F32); fence_all = dt_int("fence_all", [64, 64], F32)
    g128 = _ret_tables()[5]
    GROUPS = [[0, 1, 2, 3], [4, 5, 6, 7]]

    with ExitStack() as es:
        kb = KB(nc, es)
        ident = kb.sb(es, "ident", [128, 128], BF16); b_ident = kb.buf("ident")
        xT = kb.sb(es, "xT", [128, 16, TOK], BF16); b_xT = kb.buf("xT")
        cosF = kb.sb(es, "cosF", [128, TOK], F32); sinF = kb.sb(es, "sinF", [128, TOK], F32)
        b_cs = kb.buf("cossin")
        b_qT_d = kb.buf("qT_d"); b_kT_d = kb.buf("kT_d"); b_v_d = kb.buf("v_d"); b_z_d = kb.buf("z_d")
        b_sloc_d = kb.bufs(2, "sloc_d"); b_sloc_all = kb.bufs(2, "sloc_all")
        b_xres = [kb.buf("xres0"), kb.buf("xres1")]; b_vres = kb.buf("vres"); b_y = kb.buf("y")


        if NSA:
            kcA = kb.sb(es, "kcA", [128, 4, 256], BF16); kcB = kb.sb(es, "kcB", [64, 4, 256], BF16)
            vcx = kb.sb(es, "vcx", [128, 2, 4, 193], BF16); b_kc = kb.buf("kc")
            b_fm_d = kb.bufs(6, "fm_d"); b_fm_all = kb.bufs(6, "fm_all")
            b_vcT_d = kb.buf("vcT_d"); b_vcT_all = kb.buf("vcT_all")
            b_vs_d = kb.buf("vs_d"); b_vs_all = kb.buf("vs_all"); b_vw_d = kb.buf("vw_d"); b_vw_all = kb.buf("vw_all")
            b_qn_d = kb.buf("qn_d"); b_G_d = kb.buf("G_d")

        kb.dma("sp", ident[:], ident_in[:, :], None, b_ident)
        b_fence_d = kb.buf("fence_d"); b_fence_all = kb.buf("fence_all")
        fz = kb.sb(es, "fence_z", [16, 64], F32); b_fz = kb.buf("fence_z")
        kb.op("dve", lambda e: e.memset(fz[:], 0.0), [], [b_fz])
        kb.dma("sp", fence_d[:, :], fz[:], b_fz, b_fence_d)

        def cc_fence():
            kb.allgather(fence_d[:, :], fence_all[:, :], b_fence_d, b_fence_all, GROUPS)
            kb.barrier()

        def make_xT(ses, name):
            xb = [kb.sb(ses, f"{name}_xb{i}", [128, D], BF16) for i in range(2)]
            b_xb = kb.bufs(2, name + "_xb")
            pT = [kb.ps(ses, f"{name}_pT{i}", [128, 1024], BF16) for i in range(2)]
            b_pT = kb.bufs(2, name + "_pT")

            def run(t, src_ap, b_src):
                s = t % 2
                kb.op("act", lambda e: e.copy(xb[s][:], src_ap), [b_src], [b_xb[s]])
                for half in range(2):
                    for j in range(8):
                        c = half * 8 + j
                        kb.op("pe", lambda e: e.transpose(pT[half][:, j * 128:(j + 1) * 128],
                                                          xb[s][:, c * 128:(c + 1) * 128], ident[:]),
                              [b_xb[s], b_ident], [b_pT[half]])
                    kb.op("dve", lambda e: e.tensor_copy(
                        xT[:, half * 8:(half + 1) * 8, t * 128:(t + 1) * 128],
                        pT[half][:].rearrange("p (c t) -> p c t", c=8)), [b_pT[half]], [b_xT])
            return run

        kb.persistent_mode = False
        with ExitStack() as ses:
            xt32 = [kb.sb(ses, f"s0_x{i}", [128, D], F32) for i in range(2)]
            b_xt32 = kb.bufs(2, "s0_x")
            mk = make_xT(ses, "s0")
            for t in range(NT):
                s = t % 2
                kb.dma("sp", xt32[s][:], x_in[t * 128:(t + 1) * 128, :], None, b_xt32[s])
                mk(t, xt32[s][:], b_xt32[s])
            posi = kb.sb(ses, "s0_posi", [128, TOK], I32); b_posi = kb.buf("posi")
            invf = kb.sb(ses, "s0_invf", [128, 1], F32); b_invf = kb.buf("invf")
            ang = kb.sb(ses, "s0_ang", [128, TOK], F32); b_ang = kb.buf("ang")
            tmp = kb.sb(ses, "s0_tmp", [128, TOK], F32); b_tmp = kb.buf("tmp")
            kf = kb.sb(ses, "s0_kf", [128, TOK], F32); b_kf = kb.buf("kf")
            ki = kb.sb(ses, "s0_ki", [128, TOK], I32); b_ki = kb.buf("ki")
            kb.dma("sp", posi[:], pos_in.partition_broadcast(128).rearrange("p a t -> p (a t)"), None, b_posi)
            kb.dma("sp", invf[:], invf_in[:, :], None, b_invf)
            kb.op("dve", lambda e: e.tensor_copy(ang[:], posi[:]), [b_posi], [b_ang])
            kb.op("dve", lambda e: e.tensor_scalar(ang[:], ang[:], invf[:, 0:1], None, ALU.mult), [b_ang, b_invf], [b_ang])
            TWO_PI = 2.0 * float(np.pi)
            for which, dst in ((0, sinF), (1, cosF)):
                shift = 0.0 if which == 0 else 0.5 * float(np.pi)
                kb.op("dve", lambda e: e.tensor_scalar(tmp[:], ang[:], shift, None, ALU.add), [b_ang], [b_tmp])
                kb.op("dve", lambda e: e.tensor_scalar(kf[:], tmp[:], 1.0 / TWO_PI, None, ALU.mult), [b_tmp], [b_kf])
                kb.op("dve", lambda e: e.tensor_copy(ki[:], kf[:]), [b_kf], [b_ki])
                kb.op("dve", lambda e: e.tensor_copy(kf[:], ki[:]), [b_ki], [b_kf])
                kb.op("dve", lambda e: e.scalar_tensor_tensor(tmp[:], kf[:], -TWO_PI, tmp[:], ALU.mult, ALU.add), [b_kf, b_tmp], [b_tmp])
                kb.op("dve", lambda e: e.tensor_scalar(kf[:], tmp[:], TWO_PI, -TWO_PI, ALU.is_ge, ALU.mult), [b_tmp], [b_kf])
                kb.op("dve", lambda e: e.tensor_tensor(tmp[:], tmp[:], kf[:], ALU.add), [b_tmp, b_kf], [b_tmp])
                kb.op("dve", lambda e: e.tensor_scalar(kf[:], tmp[:], 0.0, TWO_PI, ALU.is_lt, ALU.mult), [b_tmp], [b_kf])
                kb.op("dve", lambda e: e.tensor_tensor(tmp[:], tmp[:], kf[:], ALU.add), [b_tmp, b_kf], [b_tmp])
                kb.op("dve", lambda e: e.tensor_scalar(tmp[:], tmp[:], -1.0, float(np.pi), ALU.mult, ALU.add), [b_tmp], [b_tmp])
                kb.op("dve", lambda e: e.tensor_scalar(tmp[:], tmp[:], 3.1415925, -3.1415925, ALU.min, ALU.max), [b_tmp], [b_tmp])
                kb.op("act", lambda e: e.activation(dst[:], tmp[:], AF.Sin), [b_tmp], [b_cs])
            kb.barrier()
            if debug:
                dbg_cs = nc.dram_tensor("dbg_cs", [128, 2, TOK], F32, kind="ExternalOutput").ap()
                P.dbg_names.append("dbg_cs")
                b_dbg = kb.buf("dbg")
                kb.dma("sp", dbg_cs[:, 0, :], cosF[:], b_cs, b_dbg)
                kb.dma("sp", dbg_cs[:, 1, :], sinF[:], b_cs, b_dbg)
                P.b_dbg = b_dbg

        def retention_layer(l):
            xres_in_ap = x_in if l == 0 else xres_d[(l - 1) % 2]
            b_xres_in = None if l == 0 else b_xres[(l - 1) % 2]
            xres_out_ap = xres_d[l % 2]; b_xres_out = b_xres[l % 2]
            W = w_in[l].rearrange("(c p) n -> p c n", p=128)
            with ExitStack() as ses:
                wt = [kb.sb(ses, f"ra_w{i}", [128, 16, 512], BF16) for i in range(3)]
                b_wt = kb.bufs(3, "ra_w")
                dkf = kb.sb(ses, "ra_dkf", [128, RH, NT], F32); b_dkf = kb.buf("dkf")
                kb.dma("sp", dkf[:], dkf_in[:, :, :], None, b_dkf)
                qk = [kb.sb(ses, f"ra_qk{i}", [128, 2, TOK], BF16) for i in range(2)]
                b_qk = kb.bufs(2, "ra_qk")
                tcs = [kb.sb(ses, f"ra_tc{i}", [128, 2, 512], F32) for i in range(2)]
                b_tcs = kb.bufs(2, "ra_tc")
                ktm = kb.sb(ses, "ra_ktm", [128, NT, 256], BF16); b_ktm = kb.buf("ktm")
                vh = [kb.sb(ses, f"ra_v{i}", [128, NT, 512], BF16) for i in range(2)]; b_vh = kb.bufs(2, "ra_v")
                zt = [kb.sb(ses, f"ra_z{i}", [128, 512], F32) for i in range(8)]; b_zt = kb.bufs(8, "ra_z")
                sl = kb.sb(ses, "ra_sl", [128, 2, 512], BF16); b_sl = kb.buf("sl")
                pQK = [kb.ps(ses, f"ra_pQK{i}", [128, 2, 512], F32) for i in range(2)]; b_pQK = kb.bufs(2, "pQK")
                pV = [kb.ps(ses, f"ra_pV{i}", [128, 512], F32) for i in range(2)]; b_pV = kb.bufs(2, "pV")
                pT = kb.ps(ses, "ra_pT", [128, 1024], BF16); b_pT = kb.buf("ra_pT")
                pS = kb.ps(ses, "ra_pS", [128, 512], F32); b_pS = kb.buf("ra_pS")
                wi = 0
                pvi = 0
                zi = 0

                def load_w(slot, col0, ncols, dst0=0):
                    kb.dma("pool", wt[slot][:, :, dst0:dst0 + ncols], W[:, :, col0:col0 + ncols], None, b_wt[slot])

                def wcols(h, kind):
                    if kind == 0:
                        return [(h * 256, 256, 0), (2048 + h * 256, 256, 256)]
                    if kind == 1:
                        return [(4096 + h * 512, 512, 0)]
                    return [(8192 + h * 512, 512, 0)]
                seq = [(h, kind) for h in range(RH) for kind in range(3)]

                def issue_w(i):
                    if i < len(seq):
                        h, kind = seq[i]
                        for (c0, n, d0) in wcols(h, kind):
                            load_w(i % 3, c0, n, d0)
                issue_w(0); issue_w(1)
                for i, (h, kind) in enumerate(seq):
                    issue_w(i + 2)
                    ws = i % 3
                    if kind == 0:
                        for which in range(2):
                            dst = qk[which]
                            for tg in range(2):
                                pp = pQK[(which * 2 + tg) % 2]; b_pp = b_pQK[(which * 2 + tg) % 2]
                                for half in range(2):
                                    for c in range(16):
                                        kb.op("pe", lambda e: e.matmul(
                                            pp[:, half, :], wt[ws][:, c, which * 256 + half * 128: which * 256 + half * 128 + 128],
                                            xT[:, c, tg * 512:(tg + 1) * 512], start=(c == 0), stop=(c == 15)),
                                            [b_wt[ws], b_xT], [b_pp])
                                cs_c = bc(cosF[:, tg * 512:(tg + 1) * 512], 1, 2)
                                cs_s = bc(sinF[:, tg * 512:(tg + 1) * 512], 1, 2)
                                kb.op("dve", lambda e: e.tensor_tensor(tcs[0][:], pp[:], cs_c, ALU.mult), [b_pp, b_cs], [b_tcs[0]])
                                kb.op("dve", lambda e: e.tensor_tensor(tcs[1][:], pp[:], cs_s, ALU.mult), [b_pp, b_cs], [b_tcs[1]])
                                kb.op("dve", lambda e: e.tensor_tensor(dst[:, 0, tg * 512:(tg + 1) * 512], tcs[0][:, 0, :], tcs[1][:, 1, :], ALU.subtract),
                                      [b_tcs[0], b_tcs[1]], [b_qk[which]])
                                kb.op("dve", lambda e: e.tensor_tensor(dst[:, 1, tg * 512:(tg + 1) * 512], tcs[1][:, 0, :], tcs[0][:, 1, :], ALU.add),
                                      [b_tcs[0], b_tcs[1]], [b_qk[which]])
                            dd = qT_d if which == 0 else kT_d
                            kb.dma("sp", dd[h * 256:(h + 1) * 256, :].rearrange("(a p) t -> p a t", p=128), dst[:], b_qk[which],
                                   b_qT_d if which == 0 else b_kT_d)
                        for tgrp in range(2):
                            for tt in range(4):
                                t = tgrp * 4 + tt
                                for half in range(2):
                                    kb.op("pe", lambda e: e.transpose(pT[:, (tt * 2 + half) * 128:(tt * 2 + half + 1) * 128],
                                                                      qk[1][:, half, t * 128:(t + 1) * 128], ident[:]),
                                          [b_qk[1], b_ident], [b_pT])
                            for tt in range(4):
                                t = tgrp * 4 + tt
                                kb.op("act", lambda e: e.mul(ktm[:, t, :], pT[:, tt * 256:(tt + 1) * 256], dkf[:, h, t:t + 1]),
                                      [b_pT, b_dkf], [b_ktm])
                    elif kind == 1:
                        vs = h % 2
                        for t in range(NT):
                            pp = pV[pvi % 2]; b_pp = b_pV[pvi % 2]; pvi += 1
                            for c in range(16):
                                kb.op("pe", lambda e: e.matmul(pp[:], xT[:, c, t * 128:(t + 1) * 128], wt[ws][:, c, :],
                                                               start=(c == 0), stop=(c == 15)), [b_xT, b_wt[ws]], [b_pp])
                            kb.op("act", lambda e: e.copy(vh[vs][:, t, :], pp[:]), [b_pp], [b_vh[vs]])
                        kb.dma("sp", v_d[:, h * 512:(h + 1) * 512].rearrange("(t p) e -> p t e", p=128), vh[vs][:], b_vh[vs], b_v_d)
                        for half in range(2):
                            for t in range(NT):
                                kb.op("pe", lambda e: e.matmul(pS[:], ktm[:, t, half * 128:(half + 1) * 128], vh[vs][:, t, :],
                                                               start=(t == 0), stop=(t == NT - 1)), [b_ktm, b_vh[vs]], [b_pS])
                            kb.op("act", lambda e: e.copy(sl[:, half, :], pS[:]), [b_pS], [b_sl])
                        kb.dma("sp", sloc_d[h // 4][(h % 4) * 256:(h % 4 + 1) * 256, :].rearrange("(a p) e -> p a e", p=128), sl[:], b_sl, b_sloc_d[h // 4])
                    else:
                        for t in range(NT):
                            pp = pV[pvi % 2]; b_pp = b_pV[pvi % 2]; pvi += 1
                            for c in range(16):
                                kb.op("pe", lambda e: e.matmul(pp[:], xT[:, c, t * 128:(t + 1) * 128], wt[ws][:, c, :],
                                                               start=(c == 0), stop=(c == 15)), [b_xT, b_wt[ws]], [b_pp])
                            zs = zi % 8; zi += 1
                            kb.op("act", lambda e: e.activation(zt[zs][:], pp[:], AF.Silu), [b_pp], [b_zt[zs]])
                            kb.dma("sp", z_d[t * 128:(t + 1) * 128, h * 512:(h + 1) * 512], zt[zs][:], b_zt[zs], b_z_d)
            kb.barrier()
            import os
            for c in range(2):
                if os.environ.get("NO_CC"):
                    for j in range(4):
                        kb.dma("sp", sloc_all[c][j * 1024:(j + 1) * 1024, :], sloc_d[c][:, :], b_sloc_d[c], b_sloc_all[c])
                else:
                    kb.allgather(sloc_d[c][:, :], sloc_all[c][:, :], b_sloc_d[c], b_sloc_all[c], GROUPS)
            cc_fence()

            ses_o = ExitStack()
            ogT = kb.sb(ses_o, "rb_ogT", [128, 32, TOK], BF16); b_ogT = kb.buf("ogT")
            with ExitStack() as ses:
                decayT = kb.sb(ses, "rb_decayT", [128, RH, 128], F32)
                dq = kb.sb(ses, "rb_dq", [128, RH, 128], F32)
                dkc = kb.sb(ses, "rb_dkc", [128, RH], F32)
                coef = kb.sb(ses, "rb_coef", [128, RH, 4], F32)
                b_tab = kb.buf("rb_tab")
                kb.dma("sp", decayT[:], decayT_in[:, :, :], None, b_tab)
                kb.dma("sp", dq[:], dq_in[:, :, :], None, b_tab)
                kb.dma("sp", dkc[:], dkc_in[:, :], None, b_tab)
                kb.dma("sp", coef[:], coef_in[:, :, :], None, b_tab)
                qT = [kb.sb(ses, f"rb_qT{i}", [128, 2, TOK], BF16) for i in range(2)]; b_qT = kb.bufs(2, "rb_qT")
                kT = [kb.sb(ses, f"rb_kT{i}", [128, 2, TOK], BF16) for i in range(2)]; b_kT = kb.bufs(2, "rb_kT")
                vv = [kb.sb(ses, f"rb_v{i}", [128, NT, 512], BF16) for i in range(2)]; b_vv = kb.bufs(2, "rb_v")
                sj = [kb.sb(ses, f"rb_sj{i}", [128, 2, 512], BF16) for i in range(2)] * 2; b_sj = kb.bufs(2, "rb_sj") * 2
                qd = [kb.sb(ses, f"rb_qd{i}", [128, 2, TOK], BF16) for i in range(2)]; b_qd = kb.bufs(2, "rb_qd")
                ktm = [kb.sb(ses, f"rb_ktm{i}", [128, NT, 256], BF16) for i in range(2)]; b_ktm = kb.bufs(2, "rb_ktm")
                st = [kb.sb(ses, f"rb_st{i}", [128, 2, 512], F32) for i in range(2)]; b_st = kb.bufs(2, "rb_st")
                stb = [kb.sb(ses, f"rb_stb{i}", [128, 2, 512], BF16) for i in range(2)]; b_stb = kb.bufs(2, "rb_stb")
                sTa = [kb.sb(ses, f"rb_sTa{i}", [128, NT, 128], BF16) for i in range(2)]; b_sTa = kb.bufs(2, "rb_sTa")
                zz = [kb.sb(ses, f"rb_z{i}", [128, 512], F32) for i in range(4)]; b_zz = kb.bufs(4, "rb_z")
                oc = [kb.sb(ses, f"rb_oc{i}", [128, 512], F32) for i in range(2)]; b_oc = kb.bufs(2, "rb_oc")
                og = [kb.sb(ses, f"rb_og{i}", [128, 512], BF16) for i in range(4)]; b_og = kb.bufs(4, "rb_og")
                stats = [kb.sb(ses, f"rb_stats{i}", [128, 8], F32) for i in range(2)]; b_stats = kb.bufs(2, "rb_stats")
                mv = [kb.sb(ses, f"rb_mv{i}", [128, 4], F32) for i in range(2)]; b_mv = kb.bufs(2, "rb_mv")
                pUt = [kb.ps(ses, f"rb_pU{i}", [128, 1024], F32) for i in range(2)]; b_pUt = kb.bufs(2, "rb_pU")
                pOt = [kb.ps(ses, f"rb_pO{i}", [128, 512], F32) for i in range(2)]; b_pOt = kb.bufs(2, "rb_pO")
                pT = [kb.ps(ses, f"rb_pT{i}", [128, 1024], BF16) for i in range(2)]; b_pT = kb.bufs(2, "rb_pT")
                pO = pOt; b_pO = b_pOt
                pb = [pUt[1][:, 0:512], pUt[1][:, 512:1024]]; b_pb = [b_pUt[1], b_pUt[1]]
                zi = [0]

                def load_head(h, s):
                    kb.dma("sp", qT[s][:], qT_d[h * 256:(h + 1) * 256, :].rearrange("(a p) t -> p a t", p=128), b_qT_d, b_qT[s])
                    kb.dma("sp", kT[s][:], kT_d[h * 256:(h + 1) * 256, :].rearrange("(a p) t -> p a t", p=128), b_kT_d, b_kT[s])
                    kb.dma("sp", vv[s][:], v_d[:, h * 512:(h + 1) * 512].rearrange("(t p) e -> p t e", p=128), b_v_d, b_vv[s])

                def setup(h, s):
                    st2 = st[s][:].rearrange("p a e -> p (a e)")

                    def ldj(j):
                        r0 = j * 1024 + (h % 4) * 256
                        kb.dma("sp", sj[j][:], sloc_all[h // 4][r0:r0 + 256, :].rearrange("(a p) e -> p a e", p=128), b_sloc_all[h // 4], b_sj[j])
                    ldj(0); ldj(1)
                    kb.op("dve", lambda e: e.tensor_scalar(st2, sj[0][:].rearrange("p a e -> p (a e)"), coef[:, h, 0:1], None, ALU.mult),
                          [b_sj[0], b_tab], [b_st[s]])
                    ldj(2)
                    for j in range(1, 4):
                        kb.op("dve", lambda e: e.scalar_tensor_tensor(st2, sj[j][:].rearrange("p a e -> p (a e)"), coef[:, h, j:j + 1], st2,
                                                                      ALU.mult, ALU.add), [b_sj[j], b_tab, b_st[s]], [b_st[s]])
                        if j == 1:
                            ldj(3)
                    kb.op("act", lambda e: e.copy(stb[s][:], st[s][:]), [b_st[s]], [b_stb[s]])
                    kb.op("dve", lambda e: e.tensor_tensor(qd[s][:].rearrange("p a (c t) -> p a c t", t=128),
                                                           qT[s][:].rearrange("p a (c t) -> p a c t", t=128),
                                                           bc(bc(dq[:, h, :], 1, NT), 1, 2), ALU.mult), [b_qT[s], b_tab], [b_qd[s]])
                    for tgrp in range(2):
                        pp = pT[s]; b_pp = b_pT[s]
                        for tt in range(4):
                            t = tgrp * 4 + tt
                            for half in range(2):
                                kb.op("pe", lambda e: e.transpose(pp[:, (tt * 2 + half) * 128:(tt * 2 + half + 1) * 128],
                                                                  kT[s][:, half, t * 128:(t + 1) * 128], ident[:]),
                                      [b_kT[s], b_ident], [b_pp])
                        kb.op("act", lambda e: e.mul(ktm[s][:, tgrp * 4:(tgrp + 1) * 4, :].rearrange("p t d -> p (t d)"), pp[:], dkc[:, h:h + 1]),
                              [b_pp, b_tab], [b_ktm[s]])
                    for cg in range(2):
                        ps_ = pb[cg]; b_ps = b_pb[cg]
                        for cc in range(4):
                            c = cg * 4 + cc
                            tsl = slice(c * 128, (c + 1) * 128)
                            for half in range(2):
                                kb.op("pe", lambda e: e.matmul(ps_[:, cc * 128:(cc + 1) * 128], kT[s][:, half, tsl], qT[s][:, half, tsl],
                                                               start=(half == 0 and cc == 0), stop=(half == 1)), [b_kT[s], b_qT[s]], [b_ps])
                        kb.op("dve", lambda e: e.tensor_tensor(sTa[s][:, cg * 4:(cg + 1) * 4, :], ps_.rearrange("p (c t) -> p c t", t=128),
                                                               bc(decayT[:, h, :], 1, 4), ALU.mult), [b_ps, b_tab], [b_sTa[s]])

                def step(h, s, c):
                    tsl = slice(c * 128, (c + 1) * 128)
                    kb.op("pe", lambda e: e.matmul(pO[s][:], sTa[s][:, c, :], vv[s][:, c, :], start=True, stop=False),
                          [b_sTa[s], b_vv[s]], [b_pO[s]])
                    for half in range(2):
                        kb.op("pe", lambda e: e.matmul(pO[s][:], qd[s][:, half, tsl], stb[s][:, half, :], start=False, stop=(half == 1)),
                              [b_qd[s], b_stb[s]], [b_pO[s]])
                    if c < NT - 1:
                        for half in range(2):
                            kb.op("pe", lambda e: e.matmul(pUt[s][:, half * 512:(half + 1) * 512], ktm[s][:, c, half * 128:(half + 1) * 128], vv[s][:, c, :],
                                                           start=True, stop=True), [b_ktm[s], b_vv[s]], [b_pUt[s]])
                        stf = st[s][:].rearrange("p a e -> p (a e)")
                        kb.op("dve", lambda e: e.scalar_tensor_tensor(stf, stf, float(g128[h]), pUt[s][:], ALU.mult, ALU.add),
                              [b_st[s], b_pUt[s]], [b_st[s]])
                        kb.op("act", lambda e: e.copy(stb[s][:], st[s][:]), [b_st[s]], [b_stb[s]])

                zmap = {}

                def zload(h, s, c):
                    z6 = zi[0] % 4; zi[0] += 1
                    kb.dma("sp", zz[z6][:], z_d[c * 128:(c + 1) * 128, h * 512:(h + 1) * 512], b_z_d, b_zz[z6])
                    zmap[(s, c)] = z6

                mvb = kb.sb(ses, "rb_mvb", [128, 2, 4], F32); b_mvb = kb.buf("rb_mvb")

                def norm_stats(h, s, c):
                    kb.op("act", lambda e: e.copy(oc[s][:], pO[s][:]), [b_pO[s]], [b_oc[s]])
                    kb.op("dve", lambda e: e.bn_stats(stats[s][:, 0:6], oc[s][:]), [b_oc[s]], [b_stats[s]])
                    kb.op("dve", lambda e: e.bn_aggr(mvb[:, s, 0:2], stats[s][:, 0:6]), [b_stats[s]], [b_mvb])

                def norm_rstd():
                    kb.op("dve", lambda e: e.tensor_scalar(mvb[:, :, 2:3], mvb[:, :, 1:2], LN_EPS, None, ALU.add), [b_mvb], [b_mvb])
                    kb.op("act", lambda e: e.sqrt(mvb[:, :, 3:4], mvb[:, :, 2:3]), [b_mvb], [b_mvb])
                    kb.op("dve", lambda e: e.reciprocal(mvb[:, :, 2:3], mvb[:, :, 3:4]), [b_mvb], [b_mvb])
                    kb.op("dve", lambda e: e.scalar_tensor_tensor(mvb[:, :, 3:4], mvb[:, :, 0:1], -1.0, mvb[:, :, 2:3], ALU.mult, ALU.mult), [b_mvb], [b_mvb])

                def norm_apply(h, s, c):
                    z6 = zmap[(s, c)]
                    kb.op("act", lambda e: e.activation(oc[s][:], oc[s][:], AF.Identity, bias=mvb[:, s, 3:4], scale=mvb[:, s, 2:3]),
                          [b_oc[s], b_mvb], [b_oc[s]])
                    oi = (c % 2) * 2 + s
                    kb.op("dve", lambda e: e.tensor_tensor(og[oi][:], oc[s][:], zz[z6][:], ALU.mult), [b_oc[s], b_zz[z6]], [b_og[oi]])

                def post(h, s, c):
                    tsl = slice(c * 128, (c + 1) * 128)
                    oi = (c % 2) * 2 + s
                    pp = pT[s]; b_pp = b_pT[s]
                    for j in range(4):
                        kb.op("pe", lambda e: e.transpose(pp[:, j * 128:(j + 1) * 128], og[oi][:, j * 128:(j + 1) * 128], ident[:]),
                              [b_og[oi], b_ident], [b_pp])
                    kb.op("act", lambda e: e.copy(ogT[:, h * 4:(h + 1) * 4, tsl], pp[:, 0:512].rearrange("p (j t) -> p j t", j=4)),
                          [b_pp], [b_ogT])

                load_head(0, 0); load_head(1, 1)
                for hp in range(RH // 2):
                    ha, hb = 2 * hp, 2 * hp + 1
                    setup(ha, 0); setup(hb, 1)
                    zload(ha, 0, 0); zload(hb, 1, 0)
                    for c in range(NT):
                        if c + 1 < NT:
                            zload(ha, 0, c + 1); zload(hb, 1, c + 1)
                        step(ha, 0, c); step(hb, 1, c)
                        if c > 0:
                            post(ha, 0, c - 1); post(hb, 1, c - 1)
                        norm_stats(ha, 0, c); norm_stats(hb, 1, c)
                        norm_rstd()
                        norm_apply(ha, 0, c); norm_apply(hb, 1, c)
                    if hp + 1 < RH // 2:
                        load_head(ha + 2, 0); load_head(hb + 2, 1)
                    post(ha, 0, NT - 1); post(hb, 1, NT - 1)

            kb.barrier()
            with ExitStack() as ses:
                layer_epilogue(l, ses, ogT, b_ogT, 32, w_out[l], xres_in_ap, b_xres_in, xres_out_ap, b_xres_out)
            kb.barrier()
            ses_o.close()

        def layer_epilogue(l, ses, ogT, b_ogT, nk, wo_ap, xres_in_ap, b_xres_in, xres_out_ap, b_xres_out):
            last = (l == n_layers - 1)
            Wo = wo_ap.rearrange("(c p) n -> p c n", p=128)
            NW = 256 if nk > 16 else 512
            nwo = 2
            wo = [kb.sb(ses, f"rc_wo{i}", [128, nk, NW], BF16) for i in range(nwo)]; b_wo = kb.bufs(nwo, "rc_wo")
            xr = [kb.sb(ses, f"rc_xr{i}", [128, 512], F32) for i in range(4)]; b_xr = kb.bufs(4, "rc_xr")
            vt = [kb.sb(ses, f"rc_vt{i}", [128, 512], F32) for i in range(4)]; b_vt = kb.bufs(4, "rc_vt")
            stats = kb.sb(ses, "rc_stats", [128, NT, 8, 6], F32); b_stats = kb.buf("rc_stats")
            mv = kb.sb(ses, "rc_mv", [128, NT, 4], F32); b_mv = kb.buf("rc_mv")
            G = kb.sb(ses, "rc_G", [128, D], F32); Bt = kb.sb(ses, "rc_B", [128, D], F32); b_gb = kb.buf("rc_gb")
            vrow = [kb.sb(ses, f"rc_vrow{i}", [128, D], F32) for i in range(2)]; b_vrow = kb.bufs(2, "rc_vrow")
            pY = [kb.ps(ses, f"rc_pY{i}", [128, 512], F32) for i in range(4)]; b_pY = kb.bufs(4, "rc_pY")
            mk = make_xT(ses, "rc")
            kb.dma("sp", G[:], ln_g[l].partition_broadcast(128).rearrange("p a d -> p (a d)"), None, b_gb)
            kb.dma("sp", Bt[:], ln_b[l].partition_broadcast(128).rearrange("p a d -> p (a d)"), None, b_gb)
            nct = D // NW
            kb.dma("pool", wo[0][:], Wo[:, :, 0:NW], None, b_wo[0])
            order = [(n, t) for n in range(nct) for t in range(NT)]

            def load_xr(i):
                if i < len(order):
                    n, t = order[i]
                    kb.dma("sp", xr[i % 4][:, 0:NW], xres_in_ap[t * 128:(t + 1) * 128, n * NW:(n + 1) * NW], b_xres_in, b_xr[i % 4])
            load_xr(0); load_xr(1)
            for i, (n, t) in enumerate(order):
                if t == 0 and n + 1 < nct:
                    kb.dma("pool", wo[(n + 1) % 2][:], Wo[:, :, (n + 1) * NW:(n + 2) * NW], None, b_wo[(n + 1) % 2])
                ws = n % 2
                pi = i % 4
                s3 = i % 4
                load_xr(i + 2)
                for k in range(nk):
                    kb.op("pe", lambda e: e.matmul(pY[pi][:, 0:NW], ogT[:, k, t * 128:(t + 1) * 128], wo[ws][:, k, :],
                                                   start=(k == 0), stop=(k == nk - 1)), [b_ogT, b_wo[ws]], [b_pY[pi]])
                kb.op("dve", lambda e: e.scalar_tensor_tensor(vt[s3][:, 0:NW], xr[s3][:, 0:NW], float(ALPHA), pY[pi][:, 0:NW], ALU.mult, ALU.add),
                      [b_xr[s3], b_pY[pi]], [b_vt[s3]])
                kb.op("dve", lambda e: e.bn_stats(stats[:, t, n, :], vt[s3][:, 0:NW]), [b_vt[s3]], [b_stats])
                kb.dma("sp", vres_d[t * 128:(t + 1) * 128, n * NW:(n + 1) * NW], vt[s3][:, 0:NW], b_vt[s3], b_vres)
            for t in range(NT):
                kb.op("dve", lambda e: e.bn_aggr(mv[:, t, 0:2], stats[:, t, 0:D // NW, :].rearrange("p n s -> p (n s)")), [b_stats], [b_mv])
            kb.op("dve", lambda e: e.tensor_scalar(mv[:, :, 2:3], mv[:, :, 1:2], LN_EPS, None, ALU.add), [b_mv], [b_mv])
            kb.op("act", lambda e: e.sqrt(mv[:, :, 3:4], mv[:, :, 2:3]), [b_mv], [b_mv])
            kb.op("dve", lambda e: e.reciprocal(mv[:, :, 2:3], mv[:, :, 3:4]), [b_mv], [b_mv])
            kb.op("dve", lambda e: e.scalar_tensor_tensor(mv[:, :, 3:4], mv[:, :, 0:1], -1.0, mv[:, :, 2:3], ALU.mult, ALU.mult), [b_mv], [b_mv])
            for t in range(NT):
                s = t % 2
                kb.dma("sp", vrow[s][:], vres_d[t * 128:(t + 1) * 128, :], b_vres, b_vrow[s])
                kb.op("act", lambda e: e.activation(vrow[s][:], vrow[s][:], AF.Identity, bias=mv[:, t, 3:4], scale=mv[:, t, 2:3]),
                      [b_vrow[s], b_mv], [b_vrow[s]])
                kb.op("dve", lambda e: e.tensor_tensor(vrow[s][:], vrow[s][:], G[:], ALU.mult), [b_vrow[s], b_gb], [b_vrow[s]])
                kb.op("dve", lambda e: e.tensor_tensor(vrow[s][:], vrow[s][:], Bt[:], ALU.add), [b_vrow[s], b_gb], [b_vrow[s]])
                if last:
                    kb.dma("sp", y_out[t * 128:(t + 1) * 128, :], vrow[s][:], b_vrow[s], b_y)
                else:
                    kb.dma("sp", xres_out_ap[t * 128:(t + 1) * 128, :], vrow[s][:], b_vrow[s], b_xres_out)
                    mk(t, vrow[s][:], b_vrow[s])

        KV_OFF = dict(kcmp=0, vcmp=768, ksel=1280, vsel=2048, kwin=2560, vwin=3328)
        QSCALE = 192.0 ** -0.5
        BIG = 16384.0

        def fm_gemm(ses_bufs, wt_ap, b_wt, chunks, dst_ap, b_dst, row0):
            stage, b_stage, pF, b_pF, cnt = ses_bufs
            r = row0
            for (off, nr) in chunks:
                si = cnt[0] % 2; cnt[0] += 1
                for tg in range(2):
                    pi = cnt[1] % 2; cnt[1] += 1
                    for c in range(16):
                        kb.op("pe", lambda e: e.matmul(pF[pi][0:nr, :], wt_ap[:, c, off:off + nr], xT[:, c, tg * 512:(tg + 1) * 512],
                                                       start=(c == 0), stop=(c == 15)), [b_wt, b_xT], [b_pF[pi]])
                    kb.op("act", lambda e: e.copy(stage[si][0:nr, tg * 512:(tg + 1) * 512], pF[pi][0:nr, :]), [b_pF[pi]], [b_stage[si]])
                kb.dma("sp", dst_ap[r:r + nr, :], stage[si][0:nr, :], b_stage[si], b_dst)
                r += nr

        def nsa_kv():
            Wkv = nsa_w_kv.rearrange("(c p) n -> p c n", p=128)
            with ExitStack() as ses:
                wt = [kb.sb(ses, f"kv_w{i}", [128, 16, 512], BF16) for i in range(3)]; b_wt = kb.bufs(3, "kv_w")
                stage = [kb.sb(ses, f"kv_st{i}", [128, TOK], BF16) for i in range(2)]; b_stage = kb.bufs(2, "kv_st")
                vst = [kb.sb(ses, f"kv_vst{i}", [128, 512], BF16) for i in range(2)]; b_vst = kb.bufs(2, "kv_vst")
                pF = [kb.ps(ses, f"kv_pF{i}", [128, 512], F32) for i in range(2)]; b_pF = kb.bufs(2, "kv_pF")
                pV = [kb.ps(ses, f"kv_pV{i}", [128, 512], F32) for i in range(2)]; b_pV = kb.bufs(2, "kv_pV")
                fb = (stage, b_stage, pF, b_pF, [0, 0])
                tiles = [("fm", KV_OFF["kcmp"], 384, 0), ("fm", KV_OFF["kcmp"] + 384, 384, 1), ("vc", KV_OFF["vcmp"], 512, None),
                         ("fm", KV_OFF["ksel"], 384, 2), ("fm", KV_OFF["ksel"] + 384, 384, 3), ("tm", KV_OFF["vsel"], 512, 0),
                         ("fm", KV_OFF["kwin"], 384, 4), ("fm", KV_OFF["kwin"] + 384, 384, 5), ("tm", KV_OFF["vwin"], 512, 1)]

                def issue(i):
                    if i < len(tiles):
                        _, c0, ncol, _ = tiles[i]
                        kb.dma("pool", wt[i % 3][:, :, 0:ncol], Wkv[:, :, c0:c0 + ncol], None, b_wt[i % 3])
                issue(0); issue(1)
                vi = 0
                for i, (kind, c0, ncol, di) in enumerate(tiles):
                    issue(i + 2)
                    ws = i % 3
                    if kind == "fm":
                        fm_gemm(fb, wt[ws], b_wt[ws], [(0, 128), (128, 64), (192, 128), (320, 64)], fm_d[di], b_fm_d[di], 0)
                    elif kind == "vc":
                        fm_gemm(fb, wt[ws], b_wt[ws], [(0, 128), (128, 128), (256, 128), (384, 128)], vcT_d, b_vcT_d, 0)
                    else:
                        dst = (vs_d, vw_d)[di]; b_dst = (b_vs_d, b_vw_d)[di]
                        for t in range(NT):
                            pi = vi % 2; vi += 1
                            for c in range(16):
                                kb.op("pe", lambda e: e.matmul(pV[pi][:], xT[:, c, t * 128:(t + 1) * 128], wt[ws][:, c, :],
                                                               start=(c == 0), stop=(c == 15)), [b_xT, b_wt[ws]], [b_pV[pi]])
                            kb.op("act", lambda e: e.copy(vst[pi][:], pV[pi][:]), [b_pV[pi]], [b_vst[pi]])
                            kb.dma("sp", dst[t * 128:(t + 1) * 128, :], vst[pi][:], b_vst[pi], b_dst)
            kb.barrier()
            for i in range(6):
                kb.allgather(fm_d[i][:, :], fm_all[i][:, :], b_fm_d[i], b_fm_all[i], GROUPS)
            kb.allgather(vcT_d[:, :], vcT_all[:, :], b_vcT_d, b_vcT_all, GROUPS)
            kb.allgather(vs_d[:, :], vs_all[:, :], b_vs_d, b_vs_all, GROUPS)
            kb.allgather(vw_d[:, :], vw_all[:, :], b_vw_d, b_vw_all, GROUPS)
            kb.allgather(fence_d[:, :], fence_all[:, :], b_fence_d, b_fence_all, GROUPS)

        def nsa_compress():
            with ExitStack() as ses:
                w1k = kb.sb(ses, "cp_w1k", [128, 32, 256], BF16); w1kb = kb.sb(ses, "cp_w1kb", [64, 32, 256], BF16)
                w1v = kb.sb(ses, "cp_w1v", [128, 32, 256], BF16)
                w2k = kb.sb(ses, "cp_w2k", [128, 2, 192], BF16); w2v = kb.sb(ses, "cp_w2v", [128, 2, 128], BF16)
                pekA = kb.sb(ses, "cp_pekA", [128, 32], BF16); pekB = kb.sb(ses, "cp_pekB", [64, 32], BF16); pev = kb.sb(ses, "cp_pev", [128, 32], BF16)
                b_cw = kb.buf("cp_w")
                ck1 = nsa_w_ck1.rearrange("(l d) h -> d l h", d=192)
                kb.dma("pool", w1k[:], ck1[0:128, :, :], None, b_cw)
                kb.dma("pool", w1kb[:], ck1[128:192, :, :], None, b_cw)
                kb.dma("pool", w1v[:], nsa_w_cv1.rearrange("(l d) h -> d l h", d=128), None, b_cw)
                kb.dma("pool", w2k[:], nsa_w_ck2.rearrange("(c p) n -> p c n", p=128), None, b_cw)
                kb.dma("pool", w2v[:], nsa_w_cv2.rearrange("(c p) n -> p c n", p=128), None, b_cw)
                pekT = nsa_pe_k.rearrange("l d -> d l")
                kb.dma("pool", pekA[:], pekT[0:128, :], None, b_cw, allow_slow_non_contiguous=True)
                kb.dma("pool", pekB[:], pekT[128:192, :], None, b_cw, allow_slow_non_contiguous=True)
                kb.dma("pool", pev[:], nsa_pe_v.rearrange("l d -> d l"), None, b_cw, allow_slow_non_contiguous=True)
                kA = kb.sb(ses, "cp_kA", [128, 4096], BF16); kB = kb.sb(ses, "cp_kB", [64, 4096], BF16)
                vA = kb.sb(ses, "cp_vA", [128, 4096], BF16)
                b_kA = kb.buf("cp_kA"); b_vA = kb.buf("cp_vA")
                hT = kb.sb(ses, "cp_hT", [128, 2, 256], BF16); b_hT = kb.buf("cp_hT")
                bias = kb.sb(ses, "cp_bias", [128, 4], F32); b_bias = kb.buf("cp_bias")
                pH = [kb.ps(ses, f"cp_pH{i}", [128, 512], F32) for i in range(2)]; b_pH = kb.bufs(2, "cp_pH")
                pB = kb.ps(ses, "cp_pB", [128, 512], F32); b_pB = kb.buf("cp_pB")
                pK = [kb.ps(ses, f"cp_pK{i}", [128, 512], F32) for i in range(2)]; b_pK = kb.bufs(2, "cp_pK")
                for hc in range(2):
                    n = 0
                    for l in range(32):
                        kb.op("pe", lambda e: e.matmul(pB[:, hc:hc + 1], w1k[:, l, hc * 128:(hc + 1) * 128], pekA[:, l:l + 1],
                                                       start=(n == 0), stop=False), [b_cw], [b_pB]); n += 1
                        kb.op("pe", lambda e: e.matmul(pB[:, hc:hc + 1], w1kb[:, l, hc * 128:(hc + 1) * 128], pekB[:, l:l + 1],
                                                       start=False, stop=(l == 31)), [b_cw], [b_pB])
                for hc in range(2):
                    for l in range(32):
                        kb.op("pe", lambda e: e.matmul(pB[:, 2 + hc:3 + hc], w1v[:, l, hc * 128:(hc + 1) * 128], pev[:, l:l + 1],
                                                       start=(l == 0), stop=(l == 31)), [b_cw], [b_pB])
                kb.op("act", lambda e: e.copy(bias[:], pB[:, 0:4]), [b_pB], [b_bias])
                kb.op("dve", lambda e: e.memset(vcx[:], 0.0), [], [b_kc])
                for nch in range(2):
                    kb.dma("sp", vcx[:, nch, :, 129:193], bc(ovl_in[:, nch, :], 1, 4), None, b_kc)
                kb.op("dve", lambda e: e.memset(vcx[:, :, :, 128:129], 1.0), [], [b_kc])
                ii = 0
                for g in range(4):
                    fi, gl = g // 2, g % 2
                    for qq in range(4):
                        kb.dma("sp", kA[:, qq * TOK:(qq + 1) * TOK], fm_all[fi][qq * 384 + gl * 192: qq * 384 + gl * 192 + 128, :], b_fm_all[fi], b_kA)
                        kb.dma("sp", kB[:, qq * TOK:(qq + 1) * TOK], fm_all[fi][qq * 384 + gl * 192 + 128: qq * 384 + gl * 192 + 192, :], b_fm_all[fi], b_kA)
                        kb.dma("sp", vA[:, qq * TOK:(qq + 1) * TOK], vcT_all[qq * 512 + g * 128: qq * 512 + (g + 1) * 128, :], b_vcT_all, b_vA)
                    for which in range(2):
                        for hc in range(2):
                            pp = pH[ii % 2]; b_pp = b_pH[ii % 2]; ii += 1
                            n = 0
                            tot = 64 if which == 0 else 32
                            for l in range(32):
                                srcs = [(w1k, kA, 128), (w1kb, kB, 64)] if which == 0 else [(w1v, vA, 128)]
                                for (wsrc, ksrc, kr) in srcs:
                                    rhs = bass.AP(ksrc[:].tensor, ksrc[0:kr, l:l + 1].offset, [list(ksrc[0:kr, :].ap[0]), [16, 255]])
                                    kb.op("pe", lambda e: e.matmul(pp[:, 0:255], wsrc[0:kr, l, hc * 128:(hc + 1) * 128], rhs,
                                                                   start=(n == 0), stop=(n == tot - 1)),
                                          [b_cw, b_kA if which == 0 else b_vA], [b_pp]); n += 1
                            kb.op("act", lambda e: e.activation(hT[:, hc, 0:255], pp[:, 0:255], AF.Silu, bias=bias[:, which * 2 + hc: which * 2 + hc + 1]),
                                  [b_pp, b_bias], [b_hT])
                        if which == 0:
                            for (r0, nr, dst) in ((0, 128, kcA), (128, 64, kcB)):
                                pp = pK[0]; b_pp = b_pK[0]
                                for hc in range(2):
                                    kb.op("pe", lambda e: e.matmul(pp[0:nr, 0:255], w2k[:, hc, r0:r0 + nr], hT[:, hc, 0:255],
                                                                   start=(hc == 0), stop=(hc == 1)), [b_cw, b_hT], [b_pp])
                                kb.op("act", lambda e: e.copy(dst[0:nr, g, 0:255], pp[0:nr, 0:255]), [b_pp], [b_kc])
                        else:
                            for nch in range(2):
                                nr = 128 if nch == 0 else 127
                                pp = pK[1]; b_pp = b_pK[1]
                                for hc in range(2):
                                    kb.op("pe", lambda e: e.matmul(pp[0:nr, 0:128], hT[:, hc, nch * 128: nch * 128 + nr], w2v[:, hc, :],
                                                                   start=(hc == 0), stop=(hc == 1)), [b_cw, b_hT], [b_pp])
                                kb.op("act", lambda e: e.copy(vcx[0:nr, nch, g, 0:128], pp[0:nr, 0:128]), [b_pp], [b_kc])
            kb.barrier()

        def nsa_proj(l):
            Win = nsa_w_in[l - 2].rearrange("(c p) n -> p c n", p=128)
            with ExitStack() as ses:
                wt = [kb.sb(ses, f"na_w{i}", [128, 16, 512], BF16) for i in range(3)]; b_wt = kb.bufs(3, "na_w")
                stage = [kb.sb(ses, f"na_st{i}", [128, TOK], BF16) for i in range(2)]; b_stage = kb.bufs(2, "na_st")
                pF = [kb.ps(ses, f"na_pF{i}", [128, 512], F32) for i in range(2)]; b_pF = kb.bufs(2, "na_pF")
                pV = [kb.ps(ses, f"na_pV{i}", [128, 512], F32) for i in range(4)]; b_pV = kb.bufs(4, "na_pV")
                gsig = kb.sb(ses, "na_gsig", [128, NT, 48], F32); b_gsig = kb.buf("na_gsig")
                zs = [kb.sb(ses, f"na_zs{i}", [128, 512], F32) for i in range(8)]; b_zs = kb.bufs(8, "na_zs")
                fb = (stage, b_stage, pF, b_pF, [0, 0])
                tiles = [("gate", 3072 + 6144, 48, None)] + [("q", j * 384, 384, j) for j in range(8)] + \
                        [("z", 3072 + br * 2048 + n * 512, 512, (br, n)) for br in range(3) for n in range(4)]

                def issue(i):
                    if i < len(tiles):
                        _, c0, ncol, _ = tiles[i]
                        kb.dma("pool", wt[i % 3][:, :, 0:ncol], Win[:, :, c0:c0 + ncol], None, b_wt[i % 3])
                issue(0); issue(1)
                vi = 0; zi = 0
                for i, (kind, c0, ncol, info) in enumerate(tiles):
                    issue(i + 2)
                    ws = i % 3
                    if kind == "gate":
                        for t in range(NT):
                            pi = vi % 4; vi += 1
                            for c in range(16):
                                kb.op("pe", lambda e: e.matmul(pV[pi][:, 0:48], xT[:, c, t * 128:(t + 1) * 128], wt[ws][:, c, 0:48],
                                                               start=(c == 0), stop=(c == 15)), [b_xT, b_wt[ws]], [b_pV[pi]])
                            kb.op("act", lambda e: e.activation(gsig[:, t, :], pV[pi][:, 0:48], AF.Sigmoid), [b_pV[pi]], [b_gsig])
                    elif kind == "q":
                        fm_gemm(fb, wt[ws], b_wt[ws], [(0, 128), (128, 64), (192, 128), (320, 64)], qn_d, b_qn_d, info * 384)
                    else:
                        br, n = info
                        for t in range(NT):
                            pi = vi % 4; vi += 1
                            for c in range(16):
                                kb.op("pe", lambda e: e.matmul(pV[pi][:], xT[:, c, t * 128:(t + 1) * 128], wt[ws][:, c, :],
                                                               start=(c == 0), stop=(c == 15)), [b_xT, b_wt[ws]], [b_pV[pi]])
                            z3 = zi % 8; zi += 1
                            kb.op("act", lambda e: e.activation(zs[z3][:], pV[pi][:], AF.Silu), [b_pV[pi]], [b_zs[z3]])
                            kb.op("dve", lambda e: e.tensor_tensor(zs[z3][:].rearrange("p (h e) -> p h e", h=4),
                                                                   zs[z3][:].rearrange("p (h e) -> p h e", h=4),
                                                                   bc(gsig[:, t, br * 16 + n * 4: br * 16 + n * 4 + 4], 2, 128), ALU.mult),
                                  [b_zs[z3], b_gsig], [b_zs[z3]])
                            kb.dma("sp", G_d[br][t * 128:(t + 1) * 128, n * 512:(n + 1) * 512], zs[z3][:], b_zs[z3], b_G_d)
            kb.barrier()

        def nsa_attn(l):
            xres_in_ap = xres_d[(l - 1) % 2]; b_xres_in = b_xres[(l - 1) % 2]
            xres_out_ap = xres_d[l % 2]; b_xres_out = b_xres[l % 2]
            ses_o = ExitStack()
            mixT = kb.sb(ses_o, "nb_mixT", [128, 16, TOK], BF16); b_mixT = kb.buf("nb_mixT")
            with ExitStack() as ses:
                cmaskT = kb.sb(ses, "nb_cmask", [128, 2, TOK], BF16)
                addtab = kb.sb(ses, "nb_add", [128, NT, 64], F32); valid = kb.sb(ses, "nb_valid", [128, NT, 64], F32)
                tabO = kb.sb(ses, "nb_tabO", [128, NT, 64], F32); tabL = kb.sb(ses, "nb_tabL", [128, NT, 64], F32)
                tri = kb.sb(ses, "nb_tri", [128, 128], BF16); triU = kb.sb(ses, "nb_triU", [128, 128], BF16)
                hsel = kb.sb(ses, "nb_hsel", [128, 4], F32)
                b_tab = kb.buf("nb_tab")
                for dst, src in ((cmaskT, cmask_in), (addtab, addtab_in), (valid, valid_in), (tabO, tabO_in), (tabL, tabL_in),
                                 (tri, tri_in), (triU, triU_in), (hsel, hsel_in)):
                    kb.dma("sp", dst[:], src, None, b_tab)
                KA = kb.sb(ses, "nb_KA", [128, 3072], BF16); KBf = kb.sb(ses, "nb_KB", [128, 3072], BF16)
                Vx = kb.sb(ses, "nb_Vx", [128, 24, 129], BF16)
                KAo = kb.sb(ses, "nb_KAo", [128, TOK], BF16); KBo = kb.sb(ses, "nb_KBo", [128, TOK], BF16)
                Vxo = kb.sb(ses, "nb_Vxo", [128, NT, 129], BF16)
                WA = kb.sb(ses, "nb_WA", [128, 512 + TOK], BF16); WB = kb.sb(ses, "nb_WB", [64, 512 + TOK], BF16)
                Wx = kb.sb(ses, "nb_Wx", [128, 4 + NT, 129], BF16)
                cA = kb.sb(ses, "nb_cA", [128, 4, 512], BF16); cB = kb.sb(ses, "nb_cB", [64, 4, 512], BF16)
                cV = kb.sb(ses, "nb_cV", [128, 4, 4, 129], BF16)
                b_K = kb.buf("nb_K"); b_V = kb.buf("nb_V"); b_W = kb.buf("nb_W"); b_cand = kb.buf("nb_cand")
                QA = [kb.sb(ses, f"nb_QA{i}", [128, TOK], BF16) for i in range(2)]
                QBo = [kb.sb(ses, f"nb_QBo{i}", [128, TOK], BF16) for i in range(2)]
                QBw = [kb.sb(ses, f"nb_QBw{i}", [128, TOK], BF16) for i in range(2)]
                b_Q = kb.bufs(2, "nb_Q")
                MbO = kb.sb(ses, "nb_MbO", [128, TOK], BF16); MbW = kb.sb(ses, "nb_MbW", [128, TOK], BF16); b_Mb = kb.buf("nb_Mb")
                Gh = [kb.sb(ses, f"nb_G{i}", [128, NT, 128], F32) for i in range(4)]; b_Gh = kb.bufs(4, "nb_G")
                mix = kb.sb(ses, "nb_mix", [128, 4, NT, 128], F32); b_mix = kb.buf("nb_mix")
                psel = kb.sb(ses, "nb_psel", [128, NT, 64], F32); b_psel = kb.buf("nb_psel")
                PT = [kb.sb(ses, f"nb_PT{i}", [128, TOK], BF16) for i in range(3)]; b_PT = kb.bufs(3, "nb_PT")
                rinv = [kb.sb(ses, f"nb_rinv{i}", [128, 16], F32) for i in range(2)]; b_rinv = kb.bufs(2, "nb_rinv")
                tmpo = [kb.sb(ses, f"nb_tmpo{i}", [128, 128], F32) for i in range(2)]; b_tmpo = kb.bufs(2, "nb_tmpo")
                sc = kb.sb(ses, "nb_sc", [128, 64], F32); wk = kb.sb(ses, "nb_wk", [128, 64], F32)
                m8 = kb.sb(ses, "nb_m8", [128, 24], F32); Mm = kb.sb(ses, "nb_Mm", [128, 64], F32)
                Mpad = [kb.sb(ses, f"nb_Mpad{i}", [128, 128], F32) for i in range(16)]
                ident32 = kb.sb(ses, "nb_id32", [128, 128], F32)
                kb.dma("sp", ident32[:], ident32_in[:, :], None, b_tab)
                b_tk = kb.buf("nb_tk"); b_Mpad = kb.bufs(16, "nb_Mpad")
                pS = [kb.ps(ses, f"nb_pS{i}", [128, 1024], F32) for i in range(2)]; b_pS = kb.bufs(2, "nb_pS")
                pOall = kb.ps(ses, "nb_pOall", [128, 2048], F32)
                pO = [pOall[:, i * 512:(i + 1) * 512] for i in range(4)]; b_pO = kb.bufs(4, "nb_pO")
                cnt = dict(s=0, pt=0, t=0, g=0, r=0, tm=0)
                for i in range(16):
                    kb.op("dve", lambda e: e.memset(Mpad[i][:], 0.0), [], [b_Mpad[i]])

                def oacc(qt):
                    return pO[qt // 2], b_pO[qt // 2], (qt % 2) * 256

                def segs(c0, ncol):
                    out = []
                    if c0 < 512:
                        out.append((c0, min(512, c0 + ncol)))
                    if c0 + ncol > 512:
                        out.append((max(512, c0), c0 + ncol))
                    return out

                deferred = []
                pend = []

                def flush(keep=0):
                    while len(pend) > keep:
                        pend.pop(0)()

                def attend(qa, qb, lA, lB, krB, vrhs, qlo, qhi, masks, reads, first, lastf):
                    si = cnt["s"] % 2; cnt["s"] += 1
                    pi = cnt["pt"] % 3; cnt["pt"] += 1
                    c0 = qlo * 128; ncol = (qhi - qlo + 1) * 128
                    for (x0, x1) in segs(c0, ncol):
                        kb.op("pe", lambda e: e.matmul(pS[si][:, x0:x1], lA, qa[:, x0:x1], start=True, stop=False), reads, [b_pS[si]])
                        kb.op("pe", lambda e: e.matmul(pS[si][:, x0:x1], lB, qb[0:krB, x0:x1], start=False, stop=True), reads, [b_pS[si]])
                    kb.op("act", lambda e: e.activation(PT[pi][:, c0:c0 + ncol], pS[si][:, c0:c0 + ncol], AF.Exp, scale=QSCALE), [b_pS[si]], [b_PT[pi]])
                    for qt, m in masks.items():
                        kb.op("dve", lambda e: e.tensor_tensor(PT[pi][:, qt * 128:(qt + 1) * 128], PT[pi][:, qt * 128:(qt + 1) * 128], m, ALU.mult),
                              [b_PT[pi], b_tab], [b_PT[pi]])

                    def pv():
                        for qt in range(qlo, qhi + 1):
                            acc, b_acc, co = oacc(qt)
                            kb.op("pe", lambda e: e.matmul(acc[:, co:co + 129], PT[pi][:, qt * 128:(qt + 1) * 128], vrhs,
                                                           start=(first[qt] and qt % 2 == 0), stop=lastf(qt)), [b_PT[pi]] + reads, [b_acc])
                            first[qt] = False
                    flush(1)
                    pend.append(pv)

                def finish(hl, Gt, b_Gt, add):
                    flush(0)
                    ri = cnt["r"] % 2; cnt["r"] += 1
                    rs_ap = pOall[:].rearrange("p (b o c) -> p b o c", b=4, o=2)[:, :, :, 128:129]
                    kb.op("dve", lambda e: e.tensor_scalar(rinv[ri][:, 0:8].rearrange("p (b o c) -> p b o c", b=4, o=2), rs_ap, 1e-30, None, ALU.add),
                          b_pO, [b_rinv[ri]])
                    kb.op("dve", lambda e: e.reciprocal(rinv[ri][:, 8:16], rinv[ri][:, 0:8]), [b_rinv[ri]], [b_rinv[ri]])
                    for qt in range(NT):
                        acc, b_acc, co = oacc(qt)
                        t = qt
                        if not add:
                            kb.op("dve", lambda e: e.scalar_tensor_tensor(mix[:, hl, t, :], acc[:, co:co + 128], rinv[ri][:, 8 + qt:9 + qt], Gt[:, t, :],
                                                                          ALU.mult, ALU.mult), [b_acc, b_rinv[ri], b_Gt], [b_mix])
                        else:
                            ti = cnt["tm"] % 2; cnt["tm"] += 1
                            kb.op("dve", lambda e: e.scalar_tensor_tensor(tmpo[ti][:], acc[:, co:co + 128], rinv[ri][:, 8 + qt:9 + qt], Gt[:, t, :],
                                                                          ALU.mult, ALU.mult), [b_acc, b_rinv[ri], b_Gt], [b_tmpo[ti]])
                            kb.op("dve", lambda e: e.tensor_tensor(mix[:, hl, t, :], mix[:, hl, t, :], tmpo[ti][:], ALU.add), [b_mix, b_tmpo[ti]], [b_mix])
                    return ri

                def load_G(br, h):
                    gi = cnt["g"] % 4; cnt["g"] += 1
                    kb.dma("sp", Gh[gi][:], G_d[br][:, h * 128:(h + 1) * 128].rearrange("(t p) e -> p t e", p=128), b_G_d, b_Gh[gi])
                    return Gh[gi], b_Gh[gi]

                def load_q(h, s):
                    kb.dma("sp", QA[s][:], qn_d[h * 192:h * 192 + 128, :], b_qn_d, b_Q[s])
                    kb.dma("sp", QBo[s][0:64, :], qn_d[h * 192 + 128:h * 192 + 192, :], b_qn_d, b_Q[s])
                    kb.dma("sp", QBw[s][0:64, :], qn_d[h * 192 + 128:h * 192 + 192, :], b_qn_d, b_Q[s])

                hq = 0
                for g in range(4):
                    fi, gl = g // 2, g % 2
                    rA = gl * 192
                    load_q(g * 4, hq % 2)
                    for qq in range(3):
                        kb.dma("sp", KA[:, qq * TOK:(qq + 1) * TOK], fm_all[2 + fi][qq * 384 + rA: qq * 384 + rA + 128, :], b_fm_all[2 + fi], b_K)
                        kb.dma("sp", KBf[0:64, qq * TOK:(qq + 1) * TOK], fm_all[2 + fi][qq * 384 + rA + 128: qq * 384 + rA + 192, :], b_fm_all[2 + fi], b_K)
                    kb.dma("sp", KBf[64:128, :], eall_in[:, 0:3072], None, b_K)
                    kb.dma("sp", KAo[:], fm_d[2 + fi][rA:rA + 128, :], b_fm_d[2 + fi], b_K)
                    kb.dma("sp", KBo[0:64, :], fm_d[2 + fi][rA + 128:rA + 192, :], b_fm_d[2 + fi], b_K)
                    kb.dma("sp", KBo[64:128, :], eown_in[:, :], None, b_K)
                    for qq in range(3):
                        kb.dma("sp", Vx[:, qq * 8:(qq + 1) * 8, 0:128], vs_all[qq * TOK:(qq + 1) * TOK, g * 128:(g + 1) * 128].rearrange("(t p) e -> p t e", p=128), b_vs_all, b_V)
                    kb.dma("sp", Vxo[:, :, 0:128], vs_d[:, g * 128:(g + 1) * 128].rearrange("(t p) e -> p t e", p=128), b_vs_d, b_V)
                    kb.op("dve", lambda e: e.memset(Vx[:, :, 128:129], 1.0), [], [b_V])
                    kb.op("dve", lambda e: e.memset(Vxo[:, :, 128:129], 1.0), [], [b_V])
                    kb.dma("sp", WA[:, 512:], fm_d[4 + fi][rA:rA + 128, :], b_fm_d[4 + fi], b_W)
                    kb.dma("sp", WB[:, 512:], fm_d[4 + fi][rA + 128:rA + 192, :], b_fm_d[4 + fi], b_W)
                    kb.dma("sp", Wx[:, 4:, 0:128], vw_d[:, g * 128:(g + 1) * 128].rearrange("(t p) e -> p t e", p=128), b_vw_d, b_W)
                    for qq in range(4):
                        kb.dma("sp", cA[:, qq, :], fm_all[4 + fi][qq * 384 + rA: qq * 384 + rA + 128, 512:1024], b_fm_all[4 + fi], b_cand)
                        kb.dma("sp", cB[:, qq, :], fm_all[4 + fi][qq * 384 + rA + 128: qq * 384 + rA + 192, 512:1024], b_fm_all[4 + fi], b_cand)
                        kb.dma("sp", cV[:, qq, :, 0:128], vw_all[qq * TOK + 512:(qq + 1) * TOK, g * 128:(g + 1) * 128].rearrange("(t p) e -> p t e", p=128),
                               b_vw_all, b_cand)
                    kb.op("dve", lambda e: e.memset(cV[:, :, :, 128:129], 1.0), [], [b_cand])
                    kb.op("dve", lambda e: e.memset(Wx[:, 4:, 128:129], 1.0), [], [b_W])
                    for (dst, src, np_) in ((WA[:, 0:512], lambda qq: cA[:, qq, :], 128), (WB[:, 0:512], lambda qq: cB[:, qq, :], 64),
                                            (Wx[:, 0:4, :], lambda qq: cV[:, qq, :, :], 128)):
                        kb.op("dve", lambda e: e.tensor_scalar(dst, src(0), hsel[0:np_, 0:1], None, ALU.mult), [b_cand, b_tab], [b_W])
                        for qq in range(1, 4):
                            kb.op("dve", lambda e: e.scalar_tensor_tensor(dst, src(qq), hsel[0:np_, qq:qq + 1], dst, ALU.mult, ALU.add), [b_cand, b_tab, b_W], [b_W])
                    while deferred:
                        deferred.pop(0)()
                    kb.op("dve", lambda e: e.memset(psel[:], 0.0), [], [b_psel])
                    for hl in range(4):
                        h = g * 4 + hl
                        s = hq % 2; hq += 1
                        if hl > 0:
                            load_q(h, s)
                        Gt, b_Gt = load_G(0, h)
                        for nch in range(2):
                            nr = 128 if nch == 0 else 127
                            si = cnt["s"] % 2; cnt["s"] += 1
                            pi = cnt["pt"] % 3; cnt["pt"] += 1
                            for (x0, x1) in ((0, 512), (512, 1024)):
                                kb.op("pe", lambda e: e.matmul(pS[si][0:nr, x0:x1], kcA[:, g, nch * 128:nch * 128 + nr], QA[s][:, x0:x1], start=True, stop=False),
                                      [b_kc, b_Q[s]], [b_pS[si]])
                                kb.op("pe", lambda e: e.matmul(pS[si][0:nr, x0:x1], kcB[0:64, g, nch * 128:nch * 128 + nr], QBo[s][0:64, x0:x1], start=False, stop=True),
                                      [b_kc, b_Q[s]], [b_pS[si]])
                            kb.op("act", lambda e: e.activation(PT[pi][0:nr, :], pS[si][0:nr, :], AF.Exp, scale=QSCALE), [b_pS[si]], [b_PT[pi]])
                            kb.op("dve", lambda e: e.tensor_tensor(PT[pi][0:nr, :], PT[pi][0:nr, :], cmaskT[0:nr, nch, :], ALU.mult), [b_PT[pi], b_tab], [b_PT[pi]])

                            def pvc(nr=nr, nch=nch, pi=pi):
                                for qt in range(NT):
                                    acc, b_acc, co = oacc(qt)
                                    kb.op("pe", lambda e: e.matmul(acc[:, co:co + 193], PT[pi][0:nr, qt * 128:(qt + 1) * 128], vcx[0:nr, nch, g, :],
                                                                   start=(nch == 0 and qt % 2 == 0), stop=(nch == 1)), [b_PT[pi], b_kc], [b_acc])
                            flush(1)
                            pend.append(pvc)
                        ri = finish(hl, Gt, b_Gt, add=False)
                        for qt in range(NT):
                            acc, b_acc, co = oacc(qt)
                            kb.op("dve", lambda e: e.scalar_tensor_tensor(psel[:, qt, :], acc[:, co + 129:co + 193], rinv[ri][:, 8 + qt:9 + qt], psel[:, qt, :],
                                                                          ALU.mult, ALU.add), [b_acc, b_rinv[ri], b_psel], [b_psel])
                    def topk_dve(t):
                        kb.op("dve", lambda e: e.tensor_tensor(sc[:], psel[:, t, :], valid[:, t, :], ALU.mult), [b_psel, b_tab], [b_tk])
                        kb.op("dve", lambda e: e.tensor_tensor(sc[:], sc[:], addtab[:, t, :], ALU.add), [b_tk, b_tab], [b_tk])
                        kb.op("dve", lambda e: e.max(m8[:, 0:8], sc[:]), [b_tk], [b_tk])
                        kb.op("dve", lambda e: e.match_replace(wk[:], m8[:, 0:8], sc[:], -1e30), [b_tk], [b_tk])
                        kb.op("dve", lambda e: e.max(m8[:, 8:16], wk[:]), [b_tk], [b_tk])
                        kb.op("dve", lambda e: e.tensor_reduce(m8[:, 16:17], m8[:, 8:16], mybir.AxisListType.X, ALU.min), [b_tk], [b_tk])
                        kb.op("dve", lambda e: e.tensor_scalar(Mm[:], sc[:], m8[:, 16:17], None, ALU.is_ge), [b_tk], [b_tk])
                        for which, tab in ((0, tabO), (1, tabL)):
                            mi = t * 2 + which
                            kb.op("dve", lambda e: e.tensor_tensor(wk[:], Mm[:], tab[:, t, :], ALU.mult), [b_tk, b_tab], [b_tk])
                            kb.op("dve", lambda e: e.tensor_scalar(Mpad[mi][:, 64:128], wk[:], BIG, -BIG, ALU.mult, ALU.add), [b_tk], [b_Mpad[mi]])

                    def topk_pe(t):
                        for which, dstM in ((0, MbO), (1, MbW)):
                            mi = t * 2 + which
                            si = cnt["s"] % 2; cnt["s"] += 1
                            kb.op("pe", lambda e: e.transpose(pS[si][:, 0:128], Mpad[mi][:], ident32[:]), [b_Mpad[mi], b_tab], [b_pS[si]])
                            kb.op("act", lambda e: e.copy(dstM[64:128, t * 128:(t + 1) * 128], pS[si][64:128, 0:128]), [b_pS[si]], [b_Mb])
                    for t in range(NT):
                        topk_dve(t)
                    for hl in range(4):
                        h = g * 4 + hl
                        s = hq % 2; hq += 1
                        load_q(h, s)
                        Gs, b_Gs = load_G(1, h)
                        Gw, b_Gw = load_G(2, h)
                        rQ = [b_Q[s]]
                        first = {qt: True for qt in range(NT)}
                        for wkt in range(12):
                            ilo = max(wkt - 4, 0); ihi = min(wkt, NT - 1)
                            masks = {}
                            if wkt - 4 >= 0:
                                masks[wkt - 4] = tri[:]
                            if wkt <= NT - 1:
                                masks[wkt] = triU[:]
                            attend(QA[s], QBw[s], WA[:, wkt * 128:(wkt + 1) * 128], WB[0:64, wkt * 128:(wkt + 1) * 128], 64, Wx[:, wkt, :],
                                   ilo, ihi, masks, rQ + [b_W], first, lambda qt, wkt=wkt: wkt == qt + 4)
                        finish(hl, Gw, b_Gw, add=True)
                        if hl == 0:
                            for t in range(NT):
                                topk_pe(t)
                        kb.op("act", lambda e: e.copy(QBo[s][64:128, :], MbO[64:128, :]), [b_Mb], [b_Q[s]])
                        kb.op("act", lambda e: e.copy(QBw[s][64:128, :], MbW[64:128, :]), [b_Mb], [b_Q[s]])
                        first = {qt: True for qt in range(NT)}
                        for kt in range(24):
                            attend(QA[s], QBo[s], KA[:, kt * 128:(kt + 1) * 128], KBf[:, kt * 128:(kt + 1) * 128], 128, Vx[:, kt, :], 0, NT - 1, {},
                                   rQ + [b_K, b_V], first, lambda qt: False)
                            if kt == 6:
                                while deferred:
                                    deferred.pop(0)()
                        for ktl in range(NT):
                            attend(QA[s], QBw[s], KAo[:, ktl * 128:(ktl + 1) * 128], KBo[:, ktl * 128:(ktl + 1) * 128], 128, Vxo[:, ktl, :], ktl, NT - 1,
                                   {ktl: tri[:]}, rQ + [b_K, b_V], first, lambda qt, ktl=ktl: qt == ktl)
                        finish(hl, Gs, b_Gs, add=True)
                        def emit_mix(h=h, hl=hl):
                            si = cnt["s"] % 2; cnt["s"] += 1
                            for t in range(NT):
                                kb.op("pe", lambda e: e.transpose(pS[si][:, t * 128:(t + 1) * 128], mix[:, hl, t, :], ident32[:]), [b_mix, b_tab], [b_pS[si]])
                            kb.op("act", lambda e: e.copy(mixT[:, h, :], pS[si][:, :]), [b_pS[si]], [b_mixT])
                        deferred.append(emit_mix)
                while deferred:
                    deferred.pop(0)()
            kb.barrier()
            with ExitStack() as ses:
                layer_epilogue(l, ses, mixT, b_mixT, 16, nsa_w_out[l - 2], xres_in_ap, b_xres_in, xres_out_ap, b_xres_out)
            kb.barrier()
            ses_o.close()

        for l in range(min(n_layers, 2)):
            retention_layer(l)
        if NSA:
            nsa_kv()
            nsa_proj(2)
            nsa_compress()
            nsa_attn(2)
            for l in range(3, n_layers):
                nsa_proj(l)
                nsa_attn(l)

        kb.wait_all("sp", [b_y])
        if debug:
            kb.wait_all("sp", [P.b_dbg, b_qT_d, b_kT_d, b_v_d, b_z_d, b_vres])
        P.stats = dict(cnt=dict(kb.cnt), nwait=kb.nwait, nsem=kb.nsem)
    return nc, P


def _perm_ret_w_in(w):
    w = np.asarray(w)
    out = w.copy()
    idx = np.concatenate([np.arange(0, 256, 2), np.arange(1, 256, 2)])
    for blk in range(2):
        for h in range(RH):
            base = blk * 2048 + h * 256
            out[:, base:base + 256] = w[:, base + idx]
    return out


def _nsa_tables(q):
    bf = ml_dtypes.bfloat16
    tok = np.arange(TOK)
    sg = q * TOK + tok
    n = np.arange(256)
    cm = ((16 * n[:, None] + 31) <= sg[None, :]) & (n[:, None] < 255)
    cmask = np.ascontiguousarray(cm.reshape(2, 128, TOK).transpose(1, 0, 2)).astype(np.float32).astype(bf)
    j = np.arange(64)
    cur = sg // 64
    forced = (j[None, :] == 0) | (j[None, :] == cur[:, None]) | (j[None, :] == cur[:, None] - 1)
    le = j[None, :] <= cur[:, None]
    add = np.where(forced, 1e9, np.where(le, 0.0, -1.0)).astype(np.float32)
    valid = (le & ~forced).astype(np.float32)
    tabO = (le & (j[None, :] < 16 * q)).astype(np.float32)
    tabL = le.astype(np.float32)
    tm = lambda a: np.ascontiguousarray(a.reshape(NT, 128, 64).transpose(1, 0, 2))
    k = np.arange(128)
    tri = (k[:, None] <= k[None, :]).astype(np.float32).astype(bf)
    triU = (k[:, None] > k[None, :]).astype(np.float32).astype(bf)
    hsel = np.zeros((128, 4), np.float32)
    if q > 0:
        hsel[:, q - 1] = 1.0
    kk = np.arange(4096)
    eall = (kk[None, :] // 64 == j[:, None]).astype(np.float32).astype(bf)
    kl = np.arange(TOK)
    eown = ((16 * q + kl[None, :] // 64) == j[:, None]).astype(np.float32).astype(bf)
    diff = n[:, None] - 4 * j[None, :]
    ov = np.where((diff >= 0) & (diff <= 4), np.minimum(np.minimum(diff, 4 - diff), 1) + 1, 0).astype(np.float32)
    ov[255] = 0
    ovl = np.ascontiguousarray(ov.reshape(2, 128, 64).transpose(1, 0, 2)).astype(bf)
    return dict(cmask=cmask, addtab=tm(add), valid=tm(valid), tabO=tm(tabO), tabL=tm(tabL), tri=tri, triU=triU,
                hsel=hsel, eall=eall, eown=eown, ovl=ovl)


def make_in_maps(inputs, n_layers=4):
    lg, decayT, dq, dkc, dkf, g128 = _ret_tables()
    ident = np.eye(128, dtype=np.float32).astype(ml_dtypes.bfloat16)
    invf = (1.0 / (np.float32(10000.0) ** np.linspace(0.0, 1.0, 128, dtype=np.float32))).astype(np.float32).reshape(128, 1)
    x = np.asarray(inputs["x"], dtype=np.float32)
    pos = np.asarray(inputs["positions"]).astype(np.int32)
    shared = {"ident": ident, "ident32": np.eye(128, dtype=np.float32), "invf": invf, "decayT": decayT, "dq": dq, "dkc": dkc, "dkf": dkf}
    for l in range(min(2, n_layers)):
        shared[f"rwin{l}"] = _perm_ret_w_in(inputs[f"ret_w_in_{l}"])
        shared[f"rwout{l}"] = np.ascontiguousarray(np.asarray(inputs[f"ret_w_out_{l}"], dtype=np.float32))
    for l in range(n_layers):
        shared[f"lng{l}"] = np.asarray(inputs[f"ln_g_{l}"], dtype=np.float32).reshape(1, D)
        shared[f"lnb{l}"] = np.asarray(inputs[f"ln_b_{l}"], dtype=np.float32).reshape(1, D)
    if n_layers > 2:
        for nm in ("nsa_w_kv", "nsa_pe_k", "nsa_pe_v", "nsa_w_ck1", "nsa_w_ck2", "nsa_w_cv1", "nsa_w_cv2"):
            shared[nm] = np.ascontiguousarray(np.asarray(inputs[nm], dtype=np.float32))
        for l in range(2, n_layers):
            shared[f"nwin{l}"] = np.ascontiguousarray(np.asarray(inputs[f"nsa_w_in_{l}"], dtype=np.float32))
            shared[f"nwout{l}"] = np.ascontiguousarray(np.asarray(inputs[f"nsa_w_out_{l}"], dtype=np.float32))
    maps = []
    for core in range(8):
        b, q = core // 4, core % 4
        m = dict(shared)
        if n_layers > 2:
            m.update(_nsa_tables(q))
        m["x"] = np.ascontiguousarray(x[b, q * TOK:(q + 1) * TOK, :])
        m["pos"] = np.ascontiguousarray(pos[b, q * TOK:(q + 1) * TOK]).reshape(1, TOK)
        m["coef"] = _coef(lg, q)
        maps.append(m)
    return maps


def run(inputs, n_layers=4, trace=False, debug=False):
    nc, P = build_program(n_layers, debug)
    maps = make_in_maps(inputs, n_layers)
    res = run_bass_kernel_spmd(nc, maps, core_ids=list(range(8)))
    out = np.zeros((2, 4096, D), np.float32)
    for core in range(8):
        b, q = core // 4, core % 4
        out[b, q * TOK:(q + 1) * TOK, :] = res.results[core]["y"]
    if debug:
        P.dbg = [{k: np.asarray(res.results[c][k]) for k in P.dbg_names} for c in range(8)]
    return out, P


def kernel(**inputs):
    out, _ = run(inputs, 4)
    return out
```
